# Optimizing a Trainium2 kernel written in Bass

```python
import math
import jax, jax.numpy as jnp
from jax import lax
import numpy as np

D_MODEL = 1024
BATCH = 8
SEQ = 2048
DEPTH = 2
DEC_BATCH = 128
DEC_SEQ = 4
PAST_LEN = 16384
PAGE_SIZE = 128

D_RNN = D_MODEL
LRU_HEADS = 16
LRU_BLOCK = D_RNN // LRU_HEADS
CONV_W = 4
LRU_C = 8.0
D_S5 = D_MODEL
S5_GROUP = 16
S5_GROUPS = D_S5 // S5_GROUP
S5_STATE = 64
S5_CHUNK = 128
D_FF = int(math.ceil(8 * D_MODEL / 3 / 256)) * 256
DN_ALPHA = (2 * DEPTH) ** 0.25
DN_BETA = (8 * DEPTH) ** -0.25
LN_EPS = 1e-5
N_ADA = 6
SPLITS = [D_RNN, 2 * D_RNN, 2 * D_RNN + D_S5, 2 * D_RNN + D_S5 + D_MODEL]
N_IN = 2 * D_RNN + D_S5 + 2 * D_MODEL

kernel_name = "hawk_s5_gated_hybrid_step"


def layer_norm(x, g, b):
    xf = x.astype(jnp.float32)
    mu = xf.mean(-1, keepdims=True)
    var = jnp.square(xf - mu).mean(-1, keepdims=True)
    y = (xf - mu) * lax.rsqrt(var + LN_EPS) * g.astype(jnp.float32) + b.astype(jnp.float32)
    return y.astype(x.dtype)


def causal_conv(u, buf, w, b):
    T = u.shape[1]
    full = jnp.concatenate([buf.astype(u.dtype), u], axis=1)
    y = b + sum(full[:, k:k + T] * w[k] for k in range(CONV_W))
    return y, full[:, -(CONV_W - 1):]


def _lin_op(l, r):
    a1, b1 = l
    a2, b2 = r
    return a1 * a2, a2 * b1 + b2


def _clin_op(l, r):
    a1r, a1i, b1r, b1i = l
    a2r, a2i, b2r, b2i = r
    return (a2r * a1r - a2i * a1i, a2r * a1i + a2i * a1r,
            a2r * b1r - a2i * b1i + b2r, a2r * b1i + a2i * b1r + b2i)


def rg_lru(v, h0, w_gates, b_gates, lam):
    Bn, T, _ = v.shape
    vb = v.reshape(Bn, T, LRU_HEADS, LRU_BLOCK)
    g = jnp.einsum('bthi,ghij->gbthj', vb, w_gates).reshape(2, Bn, T, D_RNN)
    gates = jax.nn.sigmoid(g.astype(jnp.float32) + b_gates.astype(jnp.float32)[:, None, None, :])
    r, i = gates[0], gates[1]
    log_a = -LRU_C * r * jax.nn.softplus(-lam.astype(jnp.float32))
    a = jnp.exp(log_a)
    bterm = jnp.sqrt(-jnp.expm1(2.0 * log_a)) * (i * v.astype(jnp.float32))
    bterm = bterm.at[:, 0].add(a[:, 0] * h0.astype(jnp.float32))
    _, h = lax.associative_scan(_lin_op, (a, bterm), axis=1)
    return h, h[:, -1]


def s5_discretise(a_re, a_im, log_dt, b_re, b_im):
    a_re = a_re.astype(jnp.float32)
    a_im = a_im.astype(jnp.float32)
    b_re = b_re.astype(jnp.float32)
    b_im = b_im.astype(jnp.float32)
    dt = jnp.exp(log_dt.astype(jnp.float32))[:, None]
    mag = jnp.exp(a_re * dt)
    abar_re, abar_im = mag * jnp.cos(a_im * dt), mag * jnp.sin(a_im * dt)
    nr, ni = abar_re - 1.0, abar_im
    den = a_re * a_re + a_im * a_im
    fr = (nr * a_re + ni * a_im) / den
    fi = (ni * a_re - nr * a_im) / den
    bb_re = fr[..., None] * b_re - fi[..., None] * b_im
    bb_im = fr[..., None] * b_im + fi[..., None] * b_re
    return abar_re, abar_im, bb_re, bb_im


def s5_scan(u, h0_re, h0_im, a_re, a_im, bb_re, bb_im, c_re, c_im):
    Bn, T, G, Q = u.shape
    L = math.gcd(T, S5_CHUNK)
    n = T // L
    uc = u.reshape(Bn, n, L, G, Q).swapaxes(0, 1)
    c_re = c_re.astype(jnp.float32)
    c_im = c_im.astype(jnp.float32)

    def body(carry, u_blk):
        hr, hi = carry
        br = jnp.einsum('blgq,gpq->blgp', u_blk, bb_re)
        bi = jnp.einsum('blgq,gpq->blgp', u_blk, bb_im)
        br = br.at[:, 0].add(a_re * hr - a_im * hi)
        bi = bi.at[:, 0].add(a_re * hi + a_im * hr)
        ar = jnp.broadcast_to(a_re, br.shape)
        ai = jnp.broadcast_to(a_im, bi.shape)
        _, _, xr, xi = lax.associative_scan(_clin_op, (ar, ai, br, bi), axis=1)
        y = jnp.einsum('blgp,gqp->blgq', xr, c_re) - jnp.einsum('blgp,gqp->blgq', xi, c_im)
        return (xr[:, -1], xi[:, -1]), y

    (hr, hi), ys = lax.scan(body, (h0_re.astype(jnp.float32), h0_im.astype(jnp.float32)), uc)
    return ys.swapaxes(0, 1).reshape(Bn, T, G, Q), hr, hi


def _layer(x, c, h0, conv0, sr0, si0, w_ada, b_ada, w_in, b_in, w_conv, b_conv,
           w_lru_gates, b_lru_gates, lru_lambda, w_lru_out, s5_a_re, s5_a_im, s5_log_dt,
           s5_b_re, s5_b_im, s5_c_re, s5_c_im, s5_d, w_s5_glu, w_out, ln1_g, ln1_b,
           w_ffn_up, w_ffn_down, ln2_g, ln2_b):
    dt = x.dtype
    Bn, T, _ = x.shape
    ada = (jax.nn.silu(c) @ w_ada + b_ada).reshape(Bn, N_ADA, D_MODEL)[:, :, None, :]
    sh1, sc1, g1, sh2, sc2, g2 = (ada[:, k] for k in range(N_ADA))
    u = x * (1 + sc1) + sh1
    z = u @ w_in + b_in
    z_lru, z_gate, z_s5, m_a, m_b = jnp.split(z, SPLITS, axis=-1)
    v, conv_new = causal_conv(z_lru, conv0, w_conv, b_conv)
    h, h_last = rg_lru(v, h0, w_lru_gates, b_lru_gates, lru_lambda)
    br_a = (h.astype(dt) * jax.nn.gelu(z_gate)) @ w_lru_out
    abr, abi, bbr, bbi = s5_discretise(s5_a_re, s5_a_im, s5_log_dt, s5_b_re, s5_b_im)
    us = z_s5.astype(jnp.float32).reshape(Bn, T, S5_GROUPS, S5_GROUP)
    ys, sr, si = s5_scan(us, sr0, si0, abr, abi, bbr, bbi, s5_c_re, s5_c_im)
    ys = ys.reshape(Bn, T, D_S5) + s5_d.astype(jnp.float32) * z_s5.astype(jnp.float32)
    gl = jax.nn.gelu(ys.astype(dt)) @ w_s5_glu
    br_b = gl[..., :D_MODEL] * jax.nn.sigmoid(gl[..., D_MODEL:])
    mixed = (jax.nn.sigmoid(m_a) * br_a + jax.nn.sigmoid(m_b) * br_b) @ w_out
    x = layer_norm(DN_ALPHA * x + g1 * mixed, ln1_g, ln1_b)
    u2 = x * (1 + sc2) + sh2
    hf = u2 @ w_ffn_up
    f = (jax.nn.silu(hf[..., :D_FF]) * hf[..., D_FF:]) @ w_ffn_down
    x = layer_norm(DN_ALPHA * x + g2 * f, ln2_g, ln2_b)
    return x, h_last.astype(dt), conv_new.astype(dt), sr.astype(dt), si.astype(dt)


def setup_inputs(seed: int = 0) -> dict:
    key = jax.random.key(seed)
    ks = iter(jax.random.split(key, 48))
    nrm = lambda shape, s: jax.random.normal(next(ks), shape, jnp.float32) * s
    d = {}
    d['x_prompt'] = nrm((BATCH, SEQ, D_MODEL), 1.0)
    d['x_sample'] = nrm((DEC_BATCH, DEC_SEQ, D_MODEL), 1.0)
    d['c_prompt'] = nrm((BATCH, D_MODEL), 1.0)
    d['c_sample'] = nrm((DEC_BATCH, D_MODEL), 1.0)
    d['state_lru_h'] = nrm((DEPTH, DEC_BATCH, D_RNN), 0.5)
    d['state_lru_conv'] = nrm((DEPTH, DEC_BATCH, CONV_W - 1, D_RNN), 1.0)
    d['state_s5_re'] = nrm((DEPTH, DEC_BATCH, S5_GROUPS, S5_STATE), 0.1)
    d['state_s5_im'] = nrm((DEPTH, DEC_BATCH, S5_GROUPS, S5_STATE), 0.1)
    d['w_ada'] = nrm((DEPTH, D_MODEL, N_ADA * D_MODEL), D_MODEL ** -0.5)
    d['b_ada'] = nrm((DEPTH, N_ADA * D_MODEL), 0.01)
    d['w_in'] = nrm((DEPTH, D_MODEL, N_IN), D_MODEL ** -0.5)
    d['b_in'] = nrm((DEPTH, N_IN), 0.01)
    d['w_conv'] = nrm((DEPTH, CONV_W, D_RNN), CONV_W ** -0.5)
    d['b_conv'] = nrm((DEPTH, D_RNN), 0.01)
    d['w_lru_gates'] = nrm((DEPTH, 2, LRU_HEADS, LRU_BLOCK, LRU_BLOCK), LRU_BLOCK ** -0.5)
    d['b_lru_gates'] = nrm((DEPTH, 2, D_RNN), 0.01)
    a_pow = jax.random.uniform(next(ks), (DEPTH, D_RNN), jnp.float32, 0.9, 0.999)
    s = a_pow ** (1.0 / LRU_C)
    d['lru_lambda'] = jnp.log(s) - jnp.log1p(-s)
    d['w_lru_out'] = nrm((DEPTH, D_RNN, D_MODEL), D_RNN ** -0.5)
    d['s5_a_re'] = -0.5 + nrm((DEPTH, S5_GROUPS, S5_STATE), 0.01)
    d['s5_a_im'] = jnp.pi * jnp.arange(S5_STATE, dtype=jnp.float32) + nrm((DEPTH, S5_GROUPS, S5_STATE), 0.01)
    d['s5_log_dt'] = jax.random.uniform(next(ks), (DEPTH, S5_GROUPS), jnp.float32, math.log(0.001), math.log(0.1))
    d['s5_b_re'] = nrm((DEPTH, S5_GROUPS, S5_STATE, S5_GROUP), (2 * S5_GROUP) ** -0.5)
    d['s5_b_im'] = nrm((DEPTH, S5_GROUPS, S5_STATE, S5_GROUP), (2 * S5_GROUP) ** -0.5)
    d['s5_c_re'] = nrm((DEPTH, S5_GROUPS, S5_GROUP, S5_STATE), (2 * S5_STATE) ** -0.5)
    d['s5_c_im'] = nrm((DEPTH, S5_GROUPS, S5_GROUP, S5_STATE), (2 * S5_STATE) ** -0.5)
    d['s5_d'] = nrm((DEPTH, D_S5), 1.0)
    d['w_s5_glu'] = nrm((DEPTH, D_S5, 2 * D_MODEL), D_S5 ** -0.5)
    d['w_out'] = nrm((DEPTH, D_MODEL, D_MODEL), D_MODEL ** -0.5 * DN_BETA)
    d['ln1_g'] = 1.0 + nrm((DEPTH, D_MODEL), 0.02)
    d['ln1_b'] = nrm((DEPTH, D_MODEL), 0.01)
    d['w_ffn_up'] = nrm((DEPTH, D_MODEL, 2 * D_FF), D_MODEL ** -0.5)
    d['w_ffn_down'] = nrm((DEPTH, D_FF, D_MODEL), D_FF ** -0.5 * DN_BETA)
    d['ln2_g'] = 1.0 + nrm((DEPTH, D_MODEL), 0.02)
    d['ln2_b'] = nrm((DEPTH, D_MODEL), 0.01)
    return d


def reference(x_prompt, x_sample, c_prompt, c_sample, state_lru_h, state_lru_conv, state_s5_re,
              state_s5_im, w_ada, b_ada, w_in, b_in, w_conv, b_conv, w_lru_gates, b_lru_gates,
              lru_lambda, w_lru_out, s5_a_re, s5_a_im, s5_log_dt, s5_b_re, s5_b_im, s5_c_re,
              s5_c_im, s5_d, w_s5_glu, w_out, ln1_g, ln1_b, w_ffn_up, w_ffn_down, ln2_g, ln2_b):
    dt = x_prompt.dtype
    Bp = x_prompt.shape[0]
    hp0 = jnp.zeros((Bp, D_RNN), dt)
    cp0 = jnp.zeros((Bp, CONV_W - 1, D_RNN), dt)
    sp0 = jnp.zeros((Bp, S5_GROUPS, S5_STATE), dt)
    yp, ys = x_prompt, x_sample
    hp, hs, cvp, cvs, srp, srs, sip, sis = [], [], [], [], [], [], [], []
    for l in range(DEPTH):
        lw = (w_ada[l], b_ada[l], w_in[l], b_in[l], w_conv[l], b_conv[l], w_lru_gates[l],
              b_lru_gates[l], lru_lambda[l], w_lru_out[l], s5_a_re[l], s5_a_im[l], s5_log_dt[l],
              s5_b_re[l], s5_b_im[l], s5_c_re[l], s5_c_im[l], s5_d[l], w_s5_glu[l], w_out[l],
              ln1_g[l], ln1_b[l], w_ffn_up[l], w_ffn_down[l], ln2_g[l], ln2_b[l])
        yp, h1, c1, r1, i1 = _layer(yp, c_prompt, hp0, cp0, sp0, sp0, *lw)
        ys, h2, c2, r2, i2 = _layer(ys, c_sample, state_lru_h[l], state_lru_conv[l],
                                    state_s5_re[l], state_s5_im[l], *lw)
        hp.append(h1); hs.append(h2); cvp.append(c1); cvs.append(c2)
        srp.append(r1); srs.append(r2); sip.append(i1); sis.append(i2)
    return (yp, ys, jnp.stack(hp), jnp.stack(hs), jnp.stack(cvp), jnp.stack(cvs),
            jnp.stack(srp), jnp.stack(srs), jnp.stack(sip), jnp.stack(sis))
```

```python
import numpy as np
import concourse.bass as bass
import concourse.mybir as mybir

F32 = mybir.dt.float32
BF16 = mybir.dt.bfloat16
AF = mybir.ActivationFunctionType
ALU = mybir.AluOpType

ENGS = ["tensor", "vector", "scalar", "gpsimd", "sync"]
_ESZ = {F32: 4, BF16: 2}


def _esz(dt):
    return _ESZ.get(dt, 4)


class _Op:
    __slots__ = ("fn", "cdeps", "ddeps", "sig", "dma", "pre")

    def __init__(self, fn):
        self.fn = fn
        self.cdeps = []
        self.ddeps = []
        self.sig = False
        self.dma = None
        self.pre = None


class Sched:
    GR = 256
    NDS = 8
    SAME_ENGINE_SYNC = True

    def __init__(self, nc):
        self.nc = nc
        self.ops = {e: [] for e in ENGS}
        self.seen = {e: {f: -1 for f in ENGS} for e in ENGS}
        self.seen_d = {e: {} for e in ENGS}
        self.lastw = {}
        self.readers = {}
        self.ndma = {e: 0 for e in ENGS}

    def _keys(self, ap):
        if isinstance(ap, tuple):
            return [ap]
        t = ap.tensor
        space = str(ap.space)
        if "DRAM" in space.upper() or "HBM" in space.upper():
            return [("d", t.name)]
        row = int(t.shape[1]) if len(t.shape) > 1 else 1
        for s in t.shape[2:]:
            row *= int(s)
        col = int(ap.offset) % row
        lo = col
        hi = col
        for (st, cnt) in list(ap.ap)[1:]:
            st = int(st); cnt = int(cnt)
            if st >= 0:
                hi += st * (cnt - 1)
            else:
                lo += st * (cnt - 1)
        esz = _esz(ap.dtype)
        if "PSUM" in space.upper():
            return [("p", t.name)]
        b0 = (lo * esz) // self.GR
        b1 = (hi * esz + esz - 1) // self.GR
        return [("s", t.name, g) for g in range(b0, b1 + 1)]

    def add(self, eng, fn, reads=(), writes=(), dma=False):
        op = _Op(fn)
        idx = len(self.ops[eng])
        toks = set()
        rkeys = []
        wkeys = []
        for a in reads:
            if a is None or isinstance(a, (int, float)):
                continue
            rkeys += self._keys(a)
        for a in writes:
            wkeys += self._keys(a)
        for k in rkeys:
            w = self.lastw.get(k)
            if w is not None:
                toks.add(w)
        for k in wkeys:
            w = self.lastw.get(k)
            if w is not None:
                toks.add(w)
            for r in self.readers.get(k, ()):
                toks.add(r)
        if dma:
            n = self.ndma[eng]
            slot = n % self.NDS
            val = 16 * (n // self.NDS + 1)
            self.ndma[eng] = n + 1
            op.dma = (slot, val)
            if n >= self.NDS:
                op.pre = (slot, val - 16)
            mytok = ("d", eng, slot, val)
        else:
            mytok = ("c", eng, idx)
        for tk in toks:
            if tk[0] == "c":
                _, f, k = tk
                if f == eng and (not self.SAME_ENGINE_SYNC or f == "tensor") and not dma:
                    continue
                if self.seen[eng][f] >= k:
                    continue
                self.seen[eng][f] = k
                self.ops[f][k].sig = True
                op.cdeps.append((f, k))
            else:
                _, q, slot, val = tk
                if self.seen_d[eng].get((q, slot), 0) >= val:
                    continue
                self.seen_d[eng][(q, slot)] = val
                op.ddeps.append((q, slot, val))
        for k in wkeys:
            self.lastw[k] = mytok
            self.readers[k] = []
        for k in rkeys:
            if k in wkeys:
                continue
            self.readers.setdefault(k, []).append(mytok)
        self.ops[eng].append(op)
        return mytok

    def final_wait(self, eng, toks):
        op = _Op(None)
        for tk in toks:
            if tk[0] == "c":
                _, f, k = tk
                self.ops[f][k].sig = True
                op.cdeps.append((f, k))
            else:
                op.ddeps.append((tk[1], tk[2], tk[3]))
        self.ops[eng].append(op)

    def emit(self):
        nc = self.nc
        import contextlib
        with contextlib.ExitStack() as es:
            csem = {e: es.enter_context(nc.semaphore("c_" + e)) for e in ENGS}
            dsem = {e: [es.enter_context(nc.semaphore("d_%s_%d" % (e, i))) for i in range(self.NDS)]
                    for e in ENGS if self.ndma[e] > 0}
            cval = {}
            for e in ENGS:
                c = 0
                vals = []
                for op in self.ops[e]:
                    if op.sig:
                        c += 1
                    vals.append(c)
                cval[e] = vals
            block = es.enter_context(nc.Block())

            def body(ename):
                def _f(eng):
                    for i, op in enumerate(self.ops[ename]):
                        if op.pre is not None:
                            eng.wait_ge(dsem[ename][op.pre[0]], op.pre[1])
                        for (f, k) in op.cdeps:
                            eng.wait_ge(csem[f], cval[f][k])
                        for (q, slot, val) in op.ddeps:
                            eng.wait_ge(dsem[q][slot], val)
                        if op.fn is None:
                            continue
                        ins = op.fn(eng)
                        if op.dma is not None:
                            ins.then_inc(dsem[ename][op.dma[0]], 16)
                        elif op.sig:
                            ins.then_inc(csem[ename], 1)
                return _f

            block.tensor(body("tensor"))
            block.vector(body("vector"))
            block.scalar(body("scalar"))
            block.gpsimd(body("gpsimd"))
            block.sync(body("sync"))

    def mm(self, out, lhsT, rhs, start=True, stop=True, tp=None):
        kw = {}
        if tp is not None:
            kw["tile_position"] = tp
        return self.add("tensor", lambda e: e.matmul(out, lhsT, rhs, start=start, stop=stop, skip_group_check=True, **kw),
                        reads=[lhsT, rhs] + ([] if start else [out]), writes=[out])

    def tr(self, out, in_, ident):
        return self.add("tensor", lambda e: e.transpose(out, in_, ident), reads=[in_, ident], writes=[out])

    def act(self, out, in_, func, bias=None, scale=None, eng="scalar"):
        kw = {}
        if bias is not None:
            kw["bias"] = bias
        if scale is not None:
            kw["scale"] = scale
        return self.add("scalar", lambda e: e.activation(out, in_, func, **kw),
                        reads=[in_, bias, scale], writes=[out])

    def tt(self, out, in0, in1, op, eng="vector"):
        return self.add(eng, lambda e: e.tensor_tensor(out, in0, in1, op), reads=[in0, in1], writes=[out])

    def ts(self, out, in0, s1, s2, op0, op1=None, eng="vector"):
        if op1 is None:
            return self.add(eng, lambda e: e.tensor_scalar(out, in0, s1, None, op0), reads=[in0, s1], writes=[out])
        return self.add(eng, lambda e: e.tensor_scalar(out, in0, s1, s2, op0, op1), reads=[in0, s1, s2], writes=[out])

    def stt(self, out, in0, scalar, in1, op0, op1, eng="vector"):
        return self.add(eng, lambda e: e.scalar_tensor_tensor(out, in0, scalar, in1, op0, op1),
                        reads=[in0, scalar, in1], writes=[out])

    def scan(self, out, d0, d1, init, op0=ALU.mult, op1=ALU.add):
        return self.add("vector", lambda e: e.tensor_tensor_scan(out, d0, d1, init, op0, op1),
                        reads=[d0, d1, init], writes=[out])

    def copy(self, out, in_, eng="vector"):
        if eng == "scalar":
            return self.act(out, in_, AF.Copy)
        return self.add(eng, lambda e: e.tensor_copy(out, in_), reads=[in_], writes=[out])

    def memset(self, out, val, eng="vector"):
        return self.add(eng, lambda e: e.memset(out, val), reads=[], writes=[out])

    def recip(self, out, in_):
        return self.add("vector", lambda e: e.reciprocal(out, in_), reads=[in_], writes=[out])

    def dma(self, out, in_, q="sync", rkey=None, wkey=None, slow=False):
        kw = {"allow_slow_non_contiguous": True} if slow else {}
        return self.add(q, lambda e: e.dma_start(out, in_, **kw),
                        reads=(list(rkey) if rkey is not None else [in_]),
                        writes=(list(wkey) if wkey is not None else [out]), dma=True)

from concourse.bass_utils import run_bass_kernel_spmd

import contextlib

NCORES = 8
DM = 1024
DEPTH = 2
NIN = 5120
DFF = 2816
ALPHA = float((2 * DEPTH) ** 0.25)
LN_EPS = 1e-5
PI = float(np.pi)


def mk(base, dims, off=0):
    p = list(base.ap)[0]
    return bass.AP(base.tensor, int(base.offset) + int(off), [[int(p[0]), int(p[1])]] + [[int(s), int(c)] for (s, c) in dims])


class Arena:
    def __init__(self, ar, total_f32):
        self.ar = ar
        self.top = 0
        self.total = total_f32 * 4
        self.jump = None

    def alloc(self, nbytes, dt=F32, at=None):
        nbytes = (nbytes + 255) // 256 * 256
        if at is None:
            at = self.top
            if self.jump is not None and at < self.jump[0] and at + nbytes > self.jump[0]:
                at = self.jump[1]
            self.top = at + nbytes
            assert self.top <= self.total, ("arena overflow", self.top, self.total)
        a = self.ar[:, at // 4:(at + nbytes) // 4]
        if dt != F32:
            a = a.bitcast(dt)
        return a


def build(nc, dbg=False):
    S = Sched(nc)
    dram = {}

    def din(name, shape):
        dram[name] = nc.dram_tensor(name, list(shape), F32, kind="ExternalInput").ap()
        return dram[name]

    def dout(name, shape):
        dram[name] = nc.dram_tensor(name, list(shape), F32, kind="ExternalOutput").ap()
        return dram[name]

    xp = din("xp", [2048, DM]); xs = din("xs", [64, DM]); call = din("call", [17, DM])
    st_h = din("st_h", [DEPTH, 16, DM]); st_cv = din("st_cv", [DEPTH, 16, 3, DM])
    st_re = din("st_re", [DEPTH, 16, 64, 64]); st_im = din("st_im", [DEPTH, 16, 64, 64])
    w_ada = din("w_ada", [DEPTH, DM, 6 * DM]); b_ada = din("b_ada", [DEPTH, 6 * DM])
    w_in = din("w_in", [DEPTH, DM, NIN]); b_in = din("b_in", [DEPTH, NIN])
    w_conv = din("w_conv", [DEPTH, 4, DM]); b_conv = din("b_conv", [DEPTH, DM])
    w_gates = din("w_lru_gates", [DEPTH, 2, 16, 64, 64]); b_gates = din("b_lru_gates", [DEPTH, 2, DM])
    lam = din("lru_lambda", [DEPTH, DM]); w_lo = din("w_lru_out", [DEPTH, DM, DM])
    a_re = din("s5_a_re", [DEPTH, 64, 64]); a_im = din("s5_a_im", [DEPTH, 64, 64]); log_dt = din("s5_log_dt", [DEPTH, 64])
    b_re = din("s5_b_re", [DEPTH, 64, 64, 16]); b_im = din("s5_b_im", [DEPTH, 64, 64, 16])
    c_re = din("s5_c_re", [DEPTH, 64, 16, 64]); c_im = din("s5_c_im", [DEPTH, 64, 16, 64])
    s5_d = din("s5_d", [DEPTH, DM]); w_glu = din("w_s5_glu", [DEPTH, DM, 2 * DM]); w_out = din("w_out", [DEPTH, DM, DM])
    ln1_g = din("ln1_g", [DEPTH, DM]); ln1_b = din("ln1_b", [DEPTH, DM])
    w_up = din("w_ffn_up", [DEPTH, DM, 2 * DFF]); w_dn = din("w_ffn_down", [DEPTH, DFF, DM])
    ln2_g = din("ln2_g", [DEPTH, DM]); ln2_b = din("ln2_b", [DEPTH, DM])
    identd = din("ident", [128, 128]); maskd = din("mask8", [128, 8])

    yp = dout("yp", [2048, DM]); ys = dout("ys", [64, DM])
    o_hp = dout("o_hp", [DEPTH, DM]); o_hs = dout("o_hs", [DEPTH, 16, DM])
    o_cp = dout("o_cp", [DEPTH, 3, DM]); o_cs = dout("o_cs", [DEPTH, 16, 3, DM])
    o_rp = dout("o_rp", [DEPTH, 64, 64]); o_rs = dout("o_rs", [DEPTH, 16, 64, 64])
    o_ip = dout("o_ip", [DEPTH, 64, 64]); o_is = dout("o_is", [DEPTH, 16, 64, 64])
    scrR = [nc.dram_tensor("scrR%d" % l, [8, 128, 8 * 4 * 2 * 128], BF16).ap() for l in range(DEPTH)]
    scrL = [nc.dram_tensor("scrL%d" % l, [8, 128, 8 * 4 * 2 * 128], BF16).ap() for l in range(DEPTH)]
    scrK = [nc.dram_tensor("scrK%d" % l, [8, 128, 8 * 128], BF16).ap() for l in range(DEPTH)]
    fin = []

    es = contextlib.ExitStack()
    TOT = 51 * 1024
    ar = es.enter_context(nc.sbuf_tensor("arena", [128, TOT], F32))
    PS = [es.enter_context(nc.psum_tensor("psb%d" % i, [128, 512], F32)) for i in range(8)]
    A = Arena(ar, TOT)
    psn = [0]

    def ps():
        psn[0] += 1
        return PS[psn[0] % 8][:, :]

    ident = A.alloc(512); S.dma(ident, identd)
    mask8 = A.alloc(32); S.dma(mask8[:, 0:8], maskd)
    onesb = A.alloc(256, BF16)
    S.memset(onesb, 1.0 / 1024.0)

    def fm(vec):
        return vec.rearrange("(t p) -> p t", p=128)

    P = []
    for l in range(DEPTH):
        d = {}
        d["pb1"] = A.alloc(88 * 4)
        d["pb2"] = A.alloc(104 * 4)
        d["b_in"] = d["pb1"][:, 0:40]
        d["b_ada"] = d["pb1"][:, 40:88]
        for i_, nm in enumerate(("b_conv", "lam", "s5_d", "ln1_g", "ln1_b", "ln2_g", "ln2_b", "bg0", "bg1")):
            d[nm] = d["pb2"][:, 8 * i_:8 * i_ + 8]
        d["wcv"] = mk(d["pb2"][:, 72:73], [(1, 8), (8, 4)])
        d["ada"] = A.alloc(48 * 17 * 4)
        d["adav"] = d["ada"][:, 0:48 * 17].rearrange("p (o s) -> p o s", s=17)
        d["wg"] = A.alloc(2 * 8 * 128 * 2, BF16)
        d["wgv"] = d["wg"][:, 0:2048].rearrange("p (g t c) -> p g t c", g=2, t=8)
        S.memset(d["wg"], 0.0)
        for g in range(2):
            for t in range(8):
                for h2 in range(2):
                    S.dma(d["wgv"][64 * h2:64 * h2 + 64, g, t, 64 * h2:64 * h2 + 64], w_gates[l, g, 2 * t + h2], q="gpsimd")
        d["clam"] = A.alloc(32)
        d["hst"] = A.alloc(32); S.memset(d["hst"], 0.0)
        d["convc"] = A.alloc(128); S.memset(d["convc"], 0.0)
        d["xst"] = A.alloc(256); S.memset(d["xst"], 0.0)
        ablk = A.alloc(1024)
        d["A8q"] = ablk[:, 0:128]
        for i_, nm in enumerate(("A4r", "A4i", "A4n")):
            d[nm] = ablk[:, 128 + 32 * i_:160 + 32 * i_]
        P.append(d)

    NMAX = 512
    bigbase = A.top
    X = A.alloc(8 * NMAX * 4)
    U = A.alloc(8 * NMAX * 2, BF16)
    ZD = A.alloc(8 * NMAX * 2, BF16)
    MA = A.alloc(8 * NMAX * 2, BF16)
    MB = A.alloc(8 * NMAX * 2, BF16)
    MIXA = A.alloc(8 * NMAX * 2, BF16)
    wb_start = A.top
    WB = [A.alloc(8 * 512 * 2, BF16) for _ in range(2)]
    CB = [A.alloc(16384, BF16) for _ in range(2)]
    KB = [A.alloc(2048, BF16) for _ in range(2)]
    STG = [A.alloc(4096) for _ in range(2)]
    regtop = A.top
    EXT = A.alloc(8 * (NMAX + 3) * 4)
    HG = A.alloc(8 * NMAX * 2, BF16)
    TMP = [A.alloc(NMAX * 4) for _ in range(4)]
    TVB = A.alloc(NMAX * 2, BF16)
    SSB = A.alloc(64 * 64 * 2, BF16)
    XIN = A.alloc(65 * 64 * 4)
    XINB = A.alloc(64 * 64 * 2, BF16)
    HH = X[:, 3072:3328]
    AD = ZD
    LNT = [XIN[:, 512 * i:512 * i + 512] for i in range(3)]
    lru_top = A.top
    s5_top = A.top
    A.top = regtop
    HF = A.alloc(22 * NMAX * 2, BF16)
    assert A.top <= lru_top - 0 and 22 * NMAX * 2 <= (8 * (NMAX + 3) * 4 + 255) // 256 * 256 + 8 * NMAX * 2
    A.top = max(lru_top, s5_top, A.top)
    print("arena used KB", A.top / 1024.0)

    rows128 = lambda v: v.rearrange("(t p) -> t p", p=128)
    for l in range(DEPTH):
        d = P[l]
        s1 = STG[0][:, 0:128]
        S.dma(s1[0:40, :], rows128(b_in[l]))
        S.dma(s1[40:88, :], rows128(b_ada[l]))
        pp = ps()
        S.tr(pp[:, 0:88], s1[0:88, :], ident[0:88, 0:88])
        S.copy(d["pb1"][:, 0:88], pp[:, 0:88])
        s2 = STG[1][:, 0:128]
        for i_, src in enumerate((b_conv[l], lam[l], s5_d[l], ln1_g[l], ln1_b[l], ln2_g[l], ln2_b[l], b_gates[l, 0], b_gates[l, 1])):
            S.dma(s2[8 * i_:8 * i_ + 8, :], rows128(src))
        for k in range(4):
            S.dma(s2[72 + 8 * k:72 + 8 * k + 8, :], rows128(w_conv[l, k]))
        pp = ps()
        S.tr(pp[:, 0:104], s2[0:104, :], ident[0:104, 0:104])
        S.copy(d["pb2"][:, 0:104], pp[:, 0:104])

    wn = [0]
    on = [0]

    def odma(out, in_, slow=False):
        on[0] += 1
        return S.dma(out, in_, wkey=[("o", on[0])], slow=slow)

    wscr = {}

    def cast_in(buf, wsrc, K, colsets):
        tot = sum(c for _, c in colsets)
        assert K * tot * 2 <= 8192
        v = buf[:, 0:K * tot].rearrange("p (k c) -> p k c", k=K)
        src = wsrc.rearrange("(k p) n -> p k n", p=128)
        o = 0
        outs = []
        for (c0, ncl) in colsets:
            S.dma(v[:, :, o:o + ncl], src[:, :, c0:c0 + ncl], q="gpsimd")
            outs.append((v, o, ncl))
            o += ncl
        return outs

    def stream(wsrc, K, colsets, key=None):
        wn[0] += 1
        buf = WB[wn[0] % 2]
        if key is None:
            return cast_in(buf, wsrc, K, colsets)
        tot = sum(c for _, c in colsets)
        scr = wscr[key]
        S.dma(buf[:, 0:K * tot], scr, q="sync", rkey=[("wscr", key)])
        v = buf[:, 0:K * tot].rearrange("p (k c) -> p k c", k=K)
        outs = []
        o = 0
        for (c0, ncl) in colsets:
            outs.append((v, o, ncl))
            o += ncl
        return outs

    def wchunks(l):
        r = []
        for ch in range(10):
            r.append((("in", l, ch), w_in[l], 8, [(512 * ch, 512)]))
        for ch in range(2):
            r.append((("lo", l, ch), w_lo[l], 8, [(512 * ch, 512)]))
        for ch in range(4):
            r.append((("glu", l, ch), w_glu[l], 8, [(256 * ch, 256), (1024 + 256 * ch, 256)]))
        for ch in range(2):
            r.append((("out", l, ch), w_out[l], 8, [(512 * ch, 512)]))
        for ch in range(11):
            r.append((("up", l, ch), w_up[l], 8, [(256 * ch, 256), (DFF + 256 * ch, 256)]))
        for ot in range(8):
            r.append((("dn", l, ot), w_dn[l], 22, [(128 * ot, 128)]))
        return r

    def convert_weights():
        bufs = [WB[0], WB[1], CB[0][:, 0:4096], CB[0][:, 4096:8192], CB[1][:, 0:4096], CB[1][:, 4096:8192]]
        allc = wchunks(0) + wchunks(1)
        LAG = 3
        pend = []
        for i, (key, wsrc, K, colsets) in enumerate(allc):
            tot = sum(c for _, c in colsets)
            wscr[key] = nc.dram_tensor("wb_%s_%d_%d" % key, [128, K * tot], BF16).ap()
            buf = bufs[i % len(bufs)]
            cast_in(buf, wsrc, K, colsets)
            pend.append((key, buf, K * tot))
            if len(pend) > LAG:
                k2, b2, n2 = pend.pop(0)
                S.dma(wscr[k2], b2[:, 0:n2], q="gpsimd", wkey=[("wscr", k2)])
        for (k2, b2, n2) in pend:
            S.dma(wscr[k2], b2[:, 0:n2], q="gpsimd", wkey=[("wscr", k2)])

    def lhs(wv, kt, j):
        v, o, ncl = wv
        return v[:, kt, o + 128 * j:o + 128 * j + 128]

    def finish():
        S.final_wait("sync", fin)
        S.emit()
        es.close()
        return nc
    if STOP == 1:
        return finish()
    if not SKIP_ADA:
      CT = TMP[0][:, 0:8 * 17].rearrange("p (k s) -> p k s", s=17)
      S.dma(STG[0][0:17, :], call)
      ppc = ps()
      for k in range(8):
          S.tr(ppc[:, 32 * k:32 * k + 17], STG[0][0:17, 128 * k:128 * k + 128], ident[0:17, 0:17])
      S.copy(CT, mk(ppc[:, 0:1], [(32, 8), (1, 17)]))
      SCB = TVB[:, 0:8 * 17].rearrange("p (k s) -> p k s", s=17)
      S.act(SCB, CT, AF.Silu)
      for l in range(DEPTH):
          for ch in range(12):
              (wv,) = stream(w_ada[l], 8, [(512 * ch, 512)])
              pp = ps()
              for j in range(4):
                  ot = 4 * ch + j
                  for kt in range(8):
                      S.mm(pp[:, 32 * j:32 * j + 17], lhs(wv, kt, j), SCB[:, kt, :], start=(kt == 0), stop=(kt == 7))
              for j in range(4):
                  ot = 4 * ch + j
                  S.act(P[l]["adav"][:, ot, :], pp[:, 32 * j:32 * j + 17], AF.Identity, bias=P[l]["b_ada"][:, ot:ot + 1])
          for idx in (1, 4):
              v = P[l]["adav"][:, 8 * idx:8 * idx + 8, :]
              S.ts(v, v, 1.0, None, ALU.add)
          t0 = TMP[1][:, 0:8]
          S.act(t0, P[l]["lam"][:, 0:8], AF.Exp, scale=-1.0)
          S.act(t0, t0, AF.Ln, bias=1.0)
          S.ts(P[l]["clam"][:, 0:8], t0, -8.0, None, ALU.mult)

    def s5_pre(l):
        d = P[l]
        A.top = bigbase
        A.jump = (wb_start, regtop)

        BIG = [A.alloc(16384) for _ in range(3)]
        SMALL = A.alloc(18 * 128)
        qn = [0]

        def q32():
            qn[0] += 1
            return SMALL[:, 32 * (qn[0] - 1):32 * qn[0]]
        AR, AI, DT, ADR, ADI, MAG, CNT, Y1, SN, CS, ABR, ABI, NR, DEN, FR, FI, T_a, T_b = [q32() for _ in range(18)]
        ast = STG[0][:, 0:128]
        for t in range(8):
            for io, srcd in enumerate((a_re, a_im)):
                S.dma(ast[32 * io + 4 * t:32 * io + 4 * t + 4, :].rearrange("p (g q) -> p g q", q=16),
                      bass.AP(srcd.tensor, int(srcd[l, 8 * t, 0:1].offset), [[16, 4], [64, 8], [1, 16]]))
        ppa = ps()
        S.tr(ppa[:, 0:64], ast[0:64, :], ident[0:64, 0:64])
        S.copy(AR, ppa[:, 0:32])
        S.copy(AI, ppa[:, 32:64])
        for gl in range(8):
            src = bass.AP(log_dt.tensor, int(log_dt[l, gl:gl + 1].offset), [[0, 16], [8, 8]])
            S.dma(DT[16 * gl:16 * gl + 16, 0:8], src, slow=True)
        S.act(DT[:, 0:8], DT[:, 0:8], AF.Exp)
        dtb = mk(DT[:, 0:8], [(1, 8), (0, 4)])
        v3 = lambda a: a.rearrange("p (t h) -> p t h", h=4)
        S.tt(v3(ADR), v3(AR), dtb, ALU.mult)
        S.tt(v3(ADI), v3(AI), dtb, ALU.mult)
        S.act(MAG, ADR, AF.Exp)

        def sin_of(out, x, shift):
            S.ts(Y1, x, shift, None, ALU.add)
            S.ts(CNT, Y1, PI, None, ALU.is_gt)
            for m in (3, 5, 7, 9, 11, 13, 15):
                S.stt(CNT, Y1, m * PI, CNT, ALU.is_gt, ALU.add)
            S.stt(Y1, CNT, -2.0 * PI, Y1, ALU.mult, ALU.add)
            S.ts(Y1, Y1, PI, -PI, ALU.min, ALU.max)
            S.act(out, Y1, AF.Sin)
        sin_of(SN, ADI, 0.0)
        sin_of(CS, ADI, PI / 2)
        S.tt(ABR, MAG, CS, ALU.mult)
        S.tt(ABI, MAG, SN, ALU.mult)
        S.ts(NR, ABR, -1.0, None, ALU.add)
        S.tt(DEN, AR, AR, ALU.mult)
        S.tt(T_a, AI, AI, ALU.mult)
        S.tt(DEN, DEN, T_a, ALU.add)
        S.recip(DEN, DEN)
        S.tt(FR, NR, AR, ALU.mult); S.tt(T_a, ABI, AI, ALU.mult); S.tt(FR, FR, T_a, ALU.add); S.tt(FR, FR, DEN, ALU.mult)
        S.tt(FI, ABI, AR, ALU.mult); S.tt(T_a, NR, AI, ALU.mult); S.tt(FI, FI, T_a, ALU.subtract); S.tt(FI, FI, DEN, ALU.mult)
        PWR = A.alloc(9 * 128)[:, 0:9 * 32].rearrange("p (k x) -> p k x", x=32)
        PWI = A.alloc(9 * 128)[:, 0:9 * 32].rearrange("p (k x) -> p k x", x=32)
        S.memset(PWR[:, 0, :], 1.0); S.memset(PWI[:, 0, :], 0.0)
        for k in range(8):
            S.tt(T_a, PWR[:, k, :], ABR, ALU.mult); S.tt(T_b, PWI[:, k, :], ABI, ALU.mult)
            S.tt(PWR[:, k + 1, :], T_a, T_b, ALU.subtract)
            S.tt(T_a, PWR[:, k, :], ABI, ALU.mult); S.tt(T_b, PWI[:, k, :], ABR, ALU.mult)
            S.tt(PWI[:, k + 1, :], T_a, T_b, ALU.add)
        for (nr_, ni_, nn_, k) in (("A4r", "A4i", "A4n", 4),):
            S.copy(d[nr_][:, 0:32], PWR[:, k, :]); S.copy(d[ni_][:, 0:32], PWI[:, k, :])
            S.ts(d[nn_][:, 0:32], PWI[:, k, :], -1.0, None, ALU.mult)
        aq = d["A8q"][:, 0:128].rearrange("p (x f) -> p x f", f=4)
        S.copy(aq[:, :, 0], PWR[:, 8, :]); S.copy(aq[:, :, 3], PWR[:, 8, :])
        S.copy(aq[:, :, 2], PWI[:, 8, :]); S.ts(aq[:, :, 1], PWI[:, 8, :], -1.0, None, ALU.mult)
        def q512():
            return A.alloc(2048)[:, 0:512].rearrange("p (x q) -> p x q", q=16)
        BR, BI, CR, CI, BBR, BBI, T1 = [q512() for _ in range(7)]
        ER, EI, T2 = BR, BI, T1
        for gl in range(8):
            for (dst, srcd) in ((BR, b_re), (BI, b_im)):
                for t in range(8):
                    S.dma(dst[16 * gl:16 * gl + 16, 4 * t:4 * t + 4, :],
                          bass.AP(srcd.tensor, int(srcd[l, 8 * t + gl, 0, 0:1].offset), [[16, 16], [256, 4], [1, 16]]))
        CQ = BIG[0]
        for (dst, srcd) in ((CR, c_re), (CI, c_im)):
            ppq = ps()
            for t in range(8):
                for ph in range(4):
                    blk = CQ[0:16, (4 * t + ph) * 128:(4 * t + ph) * 128 + 128]
                    S.dma(blk.rearrange("q (g p) -> q g p", p=16),
                          bass.AP(srcd.tensor, int(srcd[l, 8 * t, 0, 16 * ph:16 * ph + 1].offset), [[64, 16], [1024, 8], [1, 16]]))
                    S.tr(ppq[:, (4 * t + ph) * 16:(4 * t + ph) * 16 + 16], blk, ident[0:16, 0:16])
            S.copy(dst, ppq[:, 0:512].rearrange("p (x q) -> p x q", q=16))
        b16 = lambda a: mk(a, [(1, 32), (0, 16)])
        S.tt(BBR, BR, b16(FR), ALU.mult); S.tt(T1, BI, b16(FI), ALU.mult); S.tt(BBR, BBR, T1, ALU.subtract)
        S.tt(BBI, BI, b16(FR), ALU.mult); S.tt(T1, BR, b16(FI), ALU.mult); S.tt(BBI, BBI, T1, ALU.add)
        def qbd(dt=F32):
            sz = 32 * 128 * (4 if dt == F32 else 2)
            return A.alloc(sz, dt)[:, 0:4096].rearrange("p (x g q) -> p x g q", g=8, q=16)
        EBR0 = CQ[:, 0:4096].rearrange("p (x g q) -> p x g q", g=8, q=16)
        v4 = lambda a: a[:, 0:4096].rearrange("p (x g q) -> p x g q", g=8, q=16)
        EBI0, CBR = v4(BIG[1]), v4(BIG[2])
        CBN, EBR1, EBI1 = qbd(), qbd(), qbd()
        EBS = [(EBR0, EBI0), (EBR1, EBI1)]
        LBDS = [mk(EBI0[:, 0, 0, 0:1], [(1, 4096)]).bitcast(BF16)[:, 0:4096].rearrange("p (x g q) -> p x g q", g=8, q=16),
                mk(EBI1[:, 0, 0, 0:1], [(1, 4096)]).bitcast(BF16)[:, 0:4096].rearrange("p (x g q) -> p x g q", g=8, q=16)]
        RSTS = [A.alloc(4 * 128 * 2 * 2, BF16) for _ in range(2)]
        KSTS = [A.alloc(128 * 2, BF16) for _ in range(2)]
        DD = A.alloc(512)
        print("s5_pre arena top KB", A.top / 1024.0)
        mexp = mk(mask8[:, 0:8], [(0, 32), (1, 8), (0, 16)])
        ex = lambda a: mk(a, [(16, 32), (0, 8), (1, 16)])
        S.tt(CBR, ex(CR), mexp, ALU.mult)
        S.ts(T2, CI, -1.0, None, ALU.mult)
        S.tt(CBN, ex(T2), mexp, ALU.mult)
        def e_stage(s):
            k = 7 - s
            pr, pi_ = b16(PWR[:, k, :]), b16(PWI[:, k, :])
            S.tt(ER, BBR, pr, ALU.mult); S.tt(T1, BBI, pi_, ALU.mult); S.tt(ER, ER, T1, ALU.subtract)
            S.tt(EI, BBI, pr, ALU.mult); S.tt(T1, BBR, pi_, ALU.mult); S.tt(EI, EI, T1, ALU.add)
            EBR, EBI = EBS[s % 2]
            S.tt(EBR, ex(ER), mexp, ALU.mult)
            S.tt(EBI, ex(EI), mexp, ALU.mult)

        e_stage(0)
        for s in range(8):
            k = 7 - s
            if s + 1 < 8:
                e_stage(s + 1)
            EBR, EBI = EBS[s % 2]
            for t in range(8):
                pp = ps()
                pp2 = ps()
                for ph in range(4):
                    for ri, EB in enumerate((EBR, EBI)):
                        tgt = (pp if ph < 2 else pp2)[:, ((ph % 2) * 2 + ri) * 128:((ph % 2) * 2 + ri) * 128 + 128]
                        S.tr(tgt, EB[:, 4 * t + ph].rearrange("p g q -> p (g q)"), ident)
                rv = RSTS[t % 2][:, 0:1024]
                S.act(rv[:, 0:512], pp, AF.Copy)
                S.copy(rv[:, 512:1024], pp2)
                dst = scrR[l][t].rearrange("p (s x) -> p s x", s=8)[:, s, :]
                S.dma(dst, rv, wkey=[("scrR", l, t, s)])
                pk = ps()
                n = 0
                for ph in range(4):
                    for (EB, CBx) in ((EBR, CBR), (EBI, CBN)):
                        S.mm(pk[:, 0:128], EB[:, 4 * t + ph].rearrange("p g q -> p (g q)"),
                             CBx[:, 4 * t + ph].rearrange("p g q -> p (g q)"), start=(n == 0), stop=(n == 7))
                        n += 1
                kv = KSTS[t % 2][:, 0:128]
                if k == 0:
                    S.ts(DD[:, 0:128], ident, d["s5_d"][:, t:t + 1], None, ALU.mult)
                    S.tt(kv, pk[:, 0:128], DD[:, 0:128], ALU.add)
                else:
                    S.copy(kv, pk[:, 0:128])
                dstk = scrK[l][t].rearrange("p (s x) -> p s x", s=8)[:, k, :]
                S.dma(dstk, kv, wkey=[("scrK", l, t, k)])
        for tt_ in range(8):
            pr, pi_ = b16(PWR[:, tt_ + 1, :]), b16(PWI[:, tt_ + 1, :])
            S.tt(ER, CR, pr, ALU.mult); S.tt(T1, CI, pi_, ALU.mult); S.tt(ER, ER, T1, ALU.subtract)
            S.tt(EI, CI, pr, ALU.mult); S.tt(T1, CR, pi_, ALU.mult); S.tt(EI, EI, T1, ALU.add)
            S.ts(EI, EI, -1.0, None, ALU.mult)
            for ri, Ex in enumerate((ER, EI)):
                LBD = LBDS[ri]
                S.tt(LBD, ex(Ex), mexp, ALU.mult)
                for t in range(8):
                    dst = scrL[l][t].rearrange("p (s h r x) -> p s h r x", s=8, h=4, r=2)[:, tt_, :, ri, :]
                    S.dma(dst, LBD[:, 4 * t:4 * t + 4].rearrange("p h g q -> p h (g q)"), wkey=[("scrL", l, t, tt_, ri)])
        A.top = max(lru_top, s5_top)
        A.jump = None

    if STOP == 2:
        return finish()
    convert_weights()
    for l in range(DEPTH):
        if not SKIP_S5PRE:
            s5_pre(l)
    if STOP == 3:
        return finish()

    def blockrun(bi):
        sample = (bi == 4)
        N = 64 if sample else 512
        L = 4 if sample else 8
        NC_ = 16 if sample else 64
        nseq = 16 if sample else 1
        xsrc = xs if sample else xp[512 * bi:512 * bi + 512]
        ydst = ys if sample else yp[512 * bi:512 * bi + 512]
        Xv = X[:, 0:8 * N].rearrange("p (k n) -> p k n", n=N)
        Uv = U[:, 0:8 * N].rearrange("p (k n) -> p k n", n=N)
        MIXv = Uv
        ZDv = ZD[:, 0:8 * N].rearrange("p (k n) -> p k n", n=N)
        XBv = ZDv
        MAv = MA[:, 0:8 * N].rearrange("p (k n) -> p k n", n=N)
        X2Bv = MAv
        MBv = MB[:, 0:8 * N].rearrange("p (k n) -> p k n", n=N)
        MIXAv = MIXA[:, 0:8 * N].rearrange("p (k n) -> p k n", n=N)
        EW = N + 3 * nseq
        EXTv = EXT[:, 0:8 * EW].rearrange("p (k n) -> p k n", n=EW)
        HGv = HG[:, 0:8 * N].rearrange("p (k n) -> p k n", n=N)
        ADv = AD[:, 0:8 * N].rearrange("p (k n) -> p k n", n=N)
        HFv = HF[:, 0:22 * N].rearrange("p (k n) -> p k n", n=N)
        T = [t[:, 0:N] for t in TMP]
        TVBv = TVB[:, 0:N]
        LN_ = [t[:, 0:N] for t in LNT]

        def nat_of_tm(ap2d, Aa, Bb):
            return mk(ap2d, [(1, Aa), (Aa, Bb)])
        def lru_nat(ap2d):
            return nat_of_tm(ap2d, 16, 4) if sample else ap2d
        def s5_nat(ap2d):
            return nat_of_tm(ap2d, NC_, L)
        def nat3(ap2d, Aa, Bb):
            return mk(ap2d, [(Bb, Aa), (1, Bb)])

        for g in range(N // 128 if not sample else 1):
            rows = 64 if sample else 128
            st = STG[g % 2]
            S.dma(st[0:rows, :], xsrc[128 * g:128 * g + rows, :])
            for half in range(2):
                pp = ps()
                for j in range(4):
                    k = 4 * half + j
                    S.tr(pp[:, 128 * j:128 * j + rows], st[0:rows, 128 * k:128 * k + 128], ident[0:rows, 0:rows])
                for j in range(4):
                    k = 4 * half + j
                    S.copy(Xv[:, k, 128 * g:128 * g + rows], pp[:, 128 * j:128 * j + rows], eng=("scalar" if half else "vector"))

        def modulate(out3, in3, l, isc, ish):
            ada = P[l]["adav"]
            for k in range(8):
                if not sample:
                    S.ts(out3[:, k, :], in3[:, k, :], ada[:, 8 * isc + k, 0:1], ada[:, 8 * ish + k, 0:1], ALU.mult, ALU.add)
                else:
                    sc = mk(ada[:, 8 * isc + k, 1:17], [(1, 16), (0, 4)])
                    sh = mk(ada[:, 8 * ish + k, 1:17], [(1, 16), (0, 4)])
                    S.tt(T[0], in3[:, k, :], sc, ALU.mult)
                    S.tt(out3[:, k, :], T[0], sh, ALU.add)

        def gate_resid(pp, l, ig, k):
            ada = P[l]["adav"]
            if not sample:
                S.ts(T[0], pp[:, 0:N], ada[:, 8 * ig + k, 0:1], None, ALU.mult)
            else:
                S.tt(T[0], pp[:, 0:N], mk(ada[:, 8 * ig + k, 1:17], [(1, 16), (0, 4)]), ALU.mult)
            S.stt(Xv[:, k, :], Xv[:, k, :], ALPHA, T[0], ALU.mult, ALU.add)

        def layernorm(l, gname, bname):
            S.copy(XBv, Xv)
            S.act(X2Bv, Xv, AF.Square)
            pm = ps(); pq = ps()
            for k in range(8):
                S.mm(pm[:, 0:N], onesb[:, 0:128], XBv[:, k, :], start=(k == 0), stop=(k == 7))
            for k in range(8):
                S.mm(pq[:, 0:N], onesb[:, 0:128], X2Bv[:, k, :], start=(k == 0), stop=(k == 7))
            M_, V_, R_ = LN_
            S.copy(M_, pm[:, 0:N])
            S.tt(V_, M_, M_, ALU.mult)
            S.tt(V_, pq[:, 0:N], V_, ALU.subtract)
            S.ts(V_, V_, LN_EPS, None, ALU.add)
            S.act(V_, V_, AF.Sqrt)
            S.recip(R_, V_)
            for k in range(8):
                S.tt(T[0], Xv[:, k, :], M_, ALU.subtract)
                S.tt(T[0], T[0], R_, ALU.mult)
                S.ts(Xv[:, k, :], T[0], P[l][gname][:, k:k + 1], P[l][bname][:, k:k + 1], ALU.mult, ALU.add)

        for l in range(DEPTH):
            d = P[l]
            if STOP == 10:
                raise StopIteration()
            modulate(Uv, Xv, l, 1, 0)
            if STOP == 11:
                raise StopIteration()
            if sample:
                for kk in range(3):
                    S.dma(STG[0][16 * kk:16 * kk + 16, :], st_cv[l, :, kk, :])
                ppc = ps()
                for k in range(8):
                    S.tr(ppc[:, 48 * k:48 * k + 48], STG[0][0:48, 128 * k:128 * k + 128], ident[0:48, 0:48])
                S.copy(mk(EXTv[:, 0, 0:1], [(EW, 8), (1, 48)]), mk(ppc[:, 0:1], [(48, 8), (1, 48)]))
            else:
                S.copy(EXTv[:, :, 0:3], d["convc"][:, 0:24].rearrange("p (k c) -> p k c", c=3))
            if STOP == 12:
                raise StopIteration()
            SSBv = SSB[:, 0:NC_ * 64].rearrange("p (c x) -> p c x", x=64)
            XINv = XIN[:, 0:65 * 64].rearrange("p (c x) -> p c x", x=64)
            cn = [0]
            XBf = XINB[:, 0:64 * NC_].rearrange("p (x c) -> p x c", c=NC_)

            def s_phase(k):
                cn[0] += 1
                cb = CB[cn[0] % 2]
                if sample:
                    S.dma(cb[:, 4096:8192], scrR[l][k][:, 4096:8192], rkey=[("scrR", l, k, s_) for s_ in range(4, 8)])
                else:
                    S.dma(cb[:, 0:8192], scrR[l][k], rkey=[("scrR", l, k, s_) for s_ in range(8)])
                Rv = cb[:, 0:8192].rearrange("p (s h r x) -> p s h r x", s=8, h=4, r=2)
                sp = ps()
                for ph in range(4):
                    for ri in range(2):
                        o = (2 * ph + ri) * NC_
                        for s in range(L):
                            S.mm(sp[:, o:o + NC_], Rv[:, s + 8 - L, ph, ri, :], ZDv[:, k, s * NC_:(s + 1) * NC_],
                                 start=(s == 0), stop=(s == L - 1))
                S.act(mk(SSBv[:, 0, 8 * k:8 * k + 8], [(1, 8), (64, NC_)]), sp[:, 0:8 * NC_].rearrange("p (x c) -> p x c", c=NC_), AF.Copy)

            def scan_all():
                Ar, Ai, An = (d["A4r"], d["A4i"], d["A4n"])
                if not sample:
                    S.copy(XINv[:, 0, :], d["xst"][:, 0:64], eng="gpsimd")
                    p4 = STG[0][:, 0:128].rearrange("p (x a b) -> p x a b", a=2, b=2)
                    t2 = STG[1][:, 0:64].rearrange("p (x r) -> p x r", r=2)
                    aq = d["A8q"][:, 0:128].rearrange("p (x a b) -> p x a b", a=2, b=2)
                    for c in range(NC_):
                        xrep = mk(XINv[:, c, 0:1], [(2, 32), (0, 2), (1, 2)])
                        S.tt(p4, xrep, aq, ALU.mult, eng="gpsimd")
                        S.tt(t2, p4[:, :, :, 0], p4[:, :, :, 1], ALU.add, eng="gpsimd")
                        S.tt(XINv[:, c + 1, :], STG[1][:, 0:64], SSBv[:, c, :], ALU.add, eng="gpsimd")
                    S.copy(d["xst"][:, 0:64], XINv[:, NC_, :], eng="gpsimd")
                    if bi == 3:
                        xr = STG[1][:, 0:64]
                        S.copy(xr.rearrange("p (r x) -> p r x", r=2), mk(d["xst"][:, 0:1], [(1, 2), (2, 32)]))
                        ppx = ps()
                        S.tr(ppx[0:64, 0:128], xr, ident)
                        S.copy(STG[0][0:64, 0:128], ppx[0:64, 0:128])
                        for ri, od in enumerate((o_rp, o_ip)):
                            for t in range(8):
                                r0 = 32 * ri + 4 * t
                                fin.append(odma(bass.AP(od.tensor, int(od[l, 8 * t, 0:1].offset), [[16, 4], [64, 8], [1, 16]]),
                                                STG[0][r0:r0 + 4, 0:128].rearrange("p (g q) -> p g q", q=16)))
                else:
                    SS = X[:, 1024:3072]
                    for ri, sd in enumerate((st_re, st_im)):
                        for kh in range(2):
                            ppq = ps()
                            for tl in range(4):
                                for ph in range(4):
                                    bl = 4 * tl + ph
                                    blk = SS[0:16, 128 * bl:128 * bl + 128]
                                    S.dma(blk.rearrange("s (g p) -> s g p", p=16),
                                          bass.AP(sd.tensor, int(sd[l, 0, 8 * (4 * kh + tl), 16 * ph:16 * ph + 1].offset), [[4096, 16], [64, 8], [1, 16]]))
                                    S.tr(ppq[:, 16 * bl:16 * bl + 16], blk, ident[0:16, 0:16])
                            S.copy(mk(XINv[:, 0, ri:ri + 1], [(2, 16), (64, 16)], off=32 * kh), mk(ppq[:, 0:1], [(16, 16), (1, 16)]))
                    x0 = XINv[:, 0:16, :].rearrange("p s (x r) -> p s x r", r=2)
                    x1 = XINv[:, 16:32, :].rearrange("p s (x r) -> p s x r", r=2)
                    ta = mk(STG[0][:, 0:1], [(64, 16), (2, 32), (1, 2)])
                    tb = mk(STG[1][:, 0:1], [(64, 16), (2, 32), (1, 2)])
                    bc = lambda a: mk(a[:, 0:32], [(0, 16), (1, 32)])
                    S.tt(ta, x0, mk(Ar[:, 0:32], [(0, 16), (1, 32), (0, 2)]), ALU.mult)
                    S.tt(tb[:, :, :, 0], x0[:, :, :, 1], bc(An), ALU.mult)
                    S.tt(tb[:, :, :, 1], x0[:, :, :, 0], bc(Ai), ALU.mult)
                    S.tt(ta, ta, tb, ALU.add)
                    S.tt(x1, ta, SSBv.rearrange("p c (x r) -> p c x r", r=2), ALU.add)
                    for ri, od in enumerate((o_rs, o_is)):
                        for kh in range(2):
                            for tl in range(4):
                                ppq = ps()
                                for ph in range(4):
                                    cidx = ((4 * kh + tl) * 4 + ph) * 2 + ri
                                    S.tr(ppq[0:16, 128 * ph:128 * ph + 128], mk(XINv[:, 16, cidx:cidx + 1], [(64, 16)]), ident)
                                S.copy(SS[0:16, 512 * tl:512 * tl + 512], ppq[0:16, :], eng=("scalar" if tl % 2 else "vector"))
                                for ph in range(4):
                                    bl = 4 * tl + ph
                                    fin.append(odma(bass.AP(od.tensor, int(od[l, 0, 8 * (4 * kh + tl), 16 * ph:16 * ph + 1].offset), [[4096, 16], [64, 8], [1, 16]]),
                                                    SS[0:16, 128 * bl:128 * bl + 128].rearrange("s (g p) -> s g p", p=16)))
                XBf = XINB[:, 0:64 * NC_].rearrange("p (x c) -> p x c", c=NC_)
                S.copy(XBf, mk(XINv[:, 0, :], [(1, 64), (64, NC_)]), eng="gpsimd")

            def y_phase(k):
                cn[0] += 1
                cb = CB[cn[0] % 2]
                kb = KB[cn[0] % 2]
                if sample:
                    S.dma(cb[:, 0:4096], scrL[l][k][:, 0:4096], rkey=[("scrL", l, k, a_, b_) for a_ in range(4) for b_ in range(2)])
                    S.dma(kb[:, 0:512], scrK[l][k][:, 0:512], rkey=[("scrK", l, k, s_) for s_ in range(4)])
                else:
                    S.dma(cb[:, 0:8192], scrL[l][k], rkey=[("scrL", l, k, a_, b_) for a_ in range(8) for b_ in range(2)])
                    S.dma(kb[:, 0:1024], scrK[l][k], rkey=[("scrK", l, k, s_) for s_ in range(8)])
                Lv = cb[:, 0:8192].rearrange("p (s h r x) -> p s h r x", s=8, h=4, r=2)
                Kv = kb[:, 0:1024].rearrange("p (s x) -> p s x", s=8)
                yp_ = ps()
                for t in range(L):
                    o = t * NC_
                    for s in range(t + 1):
                        S.mm(yp_[:, o:o + NC_], Kv[:, t - s, :], ZDv[:, k, s * NC_:(s + 1) * NC_], start=(s == 0), stop=False)
                    for ph in range(4):
                        for ri in range(2):
                            S.mm(yp_[:, o:o + NC_], Lv[:, t, ph, ri, :], XBf[:, 8 * k + 2 * ph + ri, :], start=False,
                                 stop=(ph == 3 and ri == 1))
                S.act(ADv[:, k, :], yp_[:, 0:N], AF.Gelu_apprx_tanh)

            def inproj_chunk(ch):
                (wv,) = stream(w_in[l], 8, [(512 * ch, 512)], key=("in", l, ch))
                for j in range(4):
                    ot = 4 * ch + j
                    pp = ps()
                    for kt in range(8):
                        S.mm(pp[:, 0:N], lhs(wv, kt, j), Uv[:, kt, :], start=(kt == 0), stop=(kt == 7))
                    bias = d["b_in"][:, ot:ot + 1]
                    k = ot % 8
                    src = pp[:, 0:N]
                    if ot < 8:
                        dst = EXTv[:, k, 3 * nseq:3 * nseq + N]
                        S.act(lru_nat(dst) if sample else dst, nat3(src, 16, 4) if sample else src, AF.Identity, bias=bias)
                    elif ot < 16:
                        dst = HGv[:, k, :]
                        S.act(lru_nat(dst) if sample else dst, nat3(src, 16, 4) if sample else src, AF.Gelu_apprx_tanh, bias=bias)
                    elif ot < 24:
                        S.act(s5_nat(ZDv[:, k, :]), nat3(src, NC_, L), AF.Identity, bias=bias)
                    elif ot < 32:
                        S.act(MAv[:, k, :], src, AF.Sigmoid, bias=bias)
                    else:
                        S.act(MBv[:, k, :], src, AF.Sigmoid, bias=bias)
            inproj_chunk(4)
            inproj_chunk(5)
            for i_, ch in enumerate((0, 2, 1, 3)):
                s_phase(i_)
                inproj_chunk(ch)
            if STOP == 4:
                raise StopIteration()
            if sample:
                for half in range(2):
                    ppc = ps()
                    for j in range(4):
                        k = 4 * half + j
                        S.tr(ppc[0:48, 128 * j:128 * j + 128], EXTv[:, k, 64:112], ident)
                    S.copy(STG[1][0:48, 512 * half:512 * half + 512], ppc[0:48, :])
                for kk in range(3):
                    fin.append(odma(o_cs[l, :, kk, :], STG[1][16 * kk:16 * kk + 16, :]))
            else:
                S.copy(d["convc"][:, 0:24].rearrange("p (k c) -> p k c", c=3), EXTv[:, :, N:N + 3])
                if bi == 3:
                    ppc = ps()
                    for kk in range(3):
                        S.tr(ppc[0:8, 128 * kk:128 * kk + 128], mk(d["convc"][:, kk:kk + 1], [(3, 8)]), ident)
                    S.copy(STG[1][0:8, 0:384], ppc[0:8, 0:384])
                    for kk in range(3):
                        fin.append(odma(rows128(o_cp[l, kk]), STG[1][0:8, 128 * kk:128 * kk + 128]))
            if sample:
                H0 = HH[:, 0:128].rearrange("p (k s) -> p k s", s=16)
                S.dma(STG[0][0:16, :], st_h[l])
                pph = ps()
                for k in range(8):
                    S.tr(pph[:, 16 * k:16 * k + 16], STG[0][0:16, 128 * k:128 * k + 128], ident[0:16, 0:16])
                S.copy(H0, pph[:, 0:128].rearrange("p (k s) -> p k s", s=16))
                HOUT = HH[:, 128:256].rearrange("p (k s) -> p k s", s=16)
            TS_ = [(T, TVBv), (T, TVBv)]

            def lru_s1(k):
                Tk, TVk = TS_[k % 2]
                wc = d["wcv"]
                S.ts(Tk[0], EXTv[:, k, 0:N], wc[:, k, 0:1], d["b_conv"][:, k:k + 1], ALU.mult, ALU.add)
                for kk in range(1, 4):
                    S.stt(Tk[0], EXTv[:, k, kk * nseq:kk * nseq + N], wc[:, k, kk:kk + 1], Tk[0], ALU.mult, ALU.add)
                S.act(TVk, Tk[0], AF.Copy)
                pr = ps(); pi_ = ps()
                S.mm(pr[:, 0:N], d["wgv"][:, 0, k, :], TVk)
                S.mm(pi_[:, 0:N], d["wgv"][:, 1, k, :], TVk)
                S.act(Tk[1], pr[:, 0:N], AF.Sigmoid, bias=d["bg0"][:, k:k + 1])
                S.act(Tk[2], pi_[:, 0:N], AF.Sigmoid, bias=d["bg1"][:, k:k + 1])
                S.act(Tk[3], Tk[1], AF.Exp, scale=d["clam"][:, k:k + 1])

            def lru_s2(k):
                Tk, TVk = TS_[k % 2]
                S.tt(Tk[1], Tk[3], Tk[3], ALU.mult)
                S.act(Tk[1], Tk[1], AF.Sqrt, scale=-1.0, bias=1.0)
                S.tt(Tk[2], Tk[2], Tk[0], ALU.mult)
                S.tt(Tk[2], Tk[2], Tk[1], ALU.mult)
                if not sample:
                    S.scan(Tk[1], Tk[3], Tk[2], d["hst"][:, k:k + 1])
                    S.copy(d["hst"][:, k:k + 1], Tk[1][:, N - 1:N])
                else:
                    for t in range(4):
                        prev = H0[:, k, :] if t == 0 else Tk[1][:, 16 * (t - 1):16 * t]
                        S.tt(Tk[1][:, 16 * t:16 * t + 16], Tk[3][:, 16 * t:16 * t + 16], prev, ALU.mult)
                        S.tt(Tk[1][:, 16 * t:16 * t + 16], Tk[1][:, 16 * t:16 * t + 16], Tk[2][:, 16 * t:16 * t + 16], ALU.add)
                    S.copy(HOUT[:, k, :], Tk[1][:, 48:64])
                S.tt(HGv[:, k, :], Tk[1], HGv[:, k, :], ALU.mult)

            for k in range(8):
                lru_s1(k)
                lru_s2(k)
                if k < 2:
                    s_phase(2 * k + 4)
                    s_phase(2 * k + 5)
                if 3 <= k <= 6:
                    inproj_chunk(k + 3)
                if k == 1:
                    scan_all()
            if sample:
                for half in range(2):
                    pph = ps()
                    for j in range(4):
                        S.tr(pph[0:16, 128 * j:128 * j + 128], HOUT[:, 4 * half + j, :], ident)
                    S.copy(STG[0][0:16, 512 * half:512 * half + 512], pph[0:16, :])
                fin.append(odma(o_hs[l], STG[0][0:16, :]))
            elif bi == 3:
                pph = ps()
                S.tr(pph[0:8, 0:128], d["hst"][:, 0:8], ident)
                S.copy(STG[0][0:8, 0:128], pph[0:8, 0:128])
                fin.append(odma(rows128(o_hp[l]), STG[0][0:8, 0:128]))
            if STOP == 5:
                raise StopIteration()
            for ch in range(2):
                (wv,) = stream(w_lo[l], 8, [(512 * ch, 512)], key=("lo", l, ch))
                for j in range(4):
                    ot = 4 * ch + j
                    pp = ps()
                    for kt in range(8):
                        S.mm(pp[:, 0:N], lhs(wv, kt, j), HGv[:, kt, :], start=(kt == 0), stop=(kt == 7))
                    if sample:
                        S.tt(nat3(MIXAv[:, ot, :], 16, 4), lru_nat(pp[:, 0:N]), nat3(MAv[:, ot, :], 16, 4), ALU.mult)
                    else:
                        S.tt(MIXAv[:, ot, :], pp[:, 0:N], MAv[:, ot, :], ALU.mult)
            if STOP == 6:
                raise StopIteration()
            for k in range(8):
                y_phase(k)
            if STOP == 7:
                raise StopIteration()
            for ch in range(4):
                wv, wg_ = stream(w_glu[l], 8, [(256 * ch, 256), (1024 + 256 * ch, 256)], key=("glu", l, ch))
                for j in range(2):
                    ot = 2 * ch + j
                    pv = ps(); pg = ps()
                    for kt in range(8):
                        S.mm(pv[:, 0:N], lhs(wv, kt, j), ADv[:, kt, :], start=(kt == 0), stop=(kt == 7))
                    for kt in range(8):
                        S.mm(pg[:, 0:N], lhs(wg_, kt, j), ADv[:, kt, :], start=(kt == 0), stop=(kt == 7))
                    S.act(T[1], pg[:, 0:N], AF.Sigmoid)
                    S.tt(T[2], pv[:, 0:N], T[1], ALU.mult)
                    S.tt(nat3(T[3], NC_, L), s5_nat(T[2]), nat3(MBv[:, ot, :], NC_, L), ALU.mult)
                    S.tt(MIXv[:, ot, :], T[3], MIXAv[:, ot, :], ALU.add)
            if STOP == 8:
                raise StopIteration()
            for ch in range(2):
                (wv,) = stream(w_out[l], 8, [(512 * ch, 512)], key=("out", l, ch))
                for j in range(4):
                    ot = 4 * ch + j
                    pp = ps()
                    for kt in range(8):
                        S.mm(pp[:, 0:N], lhs(wv, kt, j), MIXv[:, kt, :], start=(kt == 0), stop=(kt == 7))
                    gate_resid(pp, l, 2, ot)
            layernorm(l, "ln1_g", "ln1_b")
            if STOP == 9:
                raise StopIteration()
            modulate(Uv, Xv, l, 4, 3)
            for ch in range(11):
                w1, w2 = stream(w_up[l], 8, [(256 * ch, 256), (DFF + 256 * ch, 256)], key=("up", l, ch))
                for j in range(2):
                    jt = 2 * ch + j
                    p1 = ps(); p2 = ps()
                    for kt in range(8):
                        S.mm(p1[:, 0:N], lhs(w1, kt, j), Uv[:, kt, :], start=(kt == 0), stop=(kt == 7))
                    for kt in range(8):
                        S.mm(p2[:, 0:N], lhs(w2, kt, j), Uv[:, kt, :], start=(kt == 0), stop=(kt == 7))
                    S.act(T[1], p1[:, 0:N], AF.Silu)
                    S.tt(HFv[:, jt, :], T[1], p2[:, 0:N], ALU.mult)
            for ot in range(8):
                ((v, _o, _n),) = stream(w_dn[l], 22, [(128 * ot, 128)], key=("dn", l, ot))
                pp = ps()
                for jt in range(22):
                    S.mm(pp[:, 0:N], v[:, jt, :], HFv[:, jt, :], start=(jt == 0), stop=(jt == 21))
                gate_resid(pp, l, 5, ot)
            layernorm(l, "ln2_g", "ln2_b")
        for g in range(N // 128 if not sample else 1):
            rows = 64 if sample else 128
            st = STG[g % 2]
            for half in range(2):
                pp = ps()
                for j in range(4):
                    k = 4 * half + j
                    S.tr(pp[0:rows, 128 * j:128 * j + 128], Xv[:, k, 128 * g:128 * g + rows], ident)
                S.copy(st[0:rows, 512 * half:512 * half + 512], pp[0:rows, :], eng=("scalar" if half else "vector"))
            fin.append(odma(ydst[128 * g:128 * g + rows, :], st[0:rows, :]))

    try:
        for bi in BLOCKS:
            blockrun(bi)
    except StopIteration:
        pass
    S.final_wait("sync", fin)
    S.emit()
    es.close()
    return nc


BLOCKS = [0, 1, 2, 3, 4]
STOP = 0
SKIP_ADA = False
SKIP_S5PRE = False
_cache = {}


def kernel(**inp):
    f32 = lambda a: np.ascontiguousarray(np.asarray(a, dtype=np.float32))
    if "nc" not in _cache:
        nc = bass.Bass("TRN2", target_bir_lowering=False)
        build(nc)
        _cache["nc"] = nc
    nc = _cache["nc"]
    shared = {}
    for nm in ("w_ada", "b_ada", "w_in", "b_in", "w_conv", "b_conv", "w_lru_gates", "b_lru_gates", "lru_lambda", "w_lru_out",
               "s5_a_re", "s5_a_im", "s5_log_dt", "s5_b_re", "s5_b_im", "s5_c_re", "s5_c_im", "s5_d", "w_s5_glu", "w_out",
               "ln1_g", "ln1_b", "w_ffn_up", "w_ffn_down", "ln2_g", "ln2_b"):
        shared[nm] = f32(inp[nm])
    shared["ident"] = np.eye(128, dtype=np.float32)
    shared["mask8"] = np.kron(np.eye(8, dtype=np.float32), np.ones((16, 1), np.float32))
    xp = f32(inp["x_prompt"]); xs = f32(inp["x_sample"]); cp = f32(inp["c_prompt"]); cs = f32(inp["c_sample"])
    sh = f32(inp["state_lru_h"]); scv = f32(inp["state_lru_conv"]); sre = f32(inp["state_s5_re"]); sim = f32(inp["state_s5_im"])
    in_maps = []
    for c in range(NCORES):
        m = dict(shared)
        sl = slice(16 * c, 16 * c + 16)
        m["xp"] = xp[c]
        m["xs"] = np.ascontiguousarray(xs[sl].reshape(64, DM))
        m["call"] = np.ascontiguousarray(np.concatenate([cp[c:c + 1], cs[sl]], axis=0))
        m["st_h"] = np.ascontiguousarray(sh[:, sl]); m["st_cv"] = np.ascontiguousarray(scv[:, sl])
        m["st_re"] = np.ascontiguousarray(sre[:, sl]); m["st_im"] = np.ascontiguousarray(sim[:, sl])
        in_maps.append(m)
    res = run_bass_kernel_spmd(nc, in_maps, core_ids=list(range(NCORES)))
    R = res.results
    cat = lambda nm, ax: np.concatenate([np.asarray(R[c][nm], dtype=np.float32) for c in range(NCORES)], axis=ax)
    stack = lambda nm, ax: np.stack([np.asarray(R[c][nm], dtype=np.float32) for c in range(NCORES)], axis=ax)
    y_p = stack("yp", 0)
    y_s = cat("ys", 0).reshape(128, 4, DM)
    return (y_p, y_s, stack("o_hp", 1), cat("o_hs", 1), stack("o_cp", 1), cat("o_cs", 1),
            stack("o_rp", 1), cat("o_rs", 1), stack("o_ip", 1), cat("o_is", 1))
```

```python
import numpy as np
import concourse.bass as bass
import concourse.mybir as mybir

F32 = mybir.dt.float32
BF16 = mybir.dt.bfloat16
AF = mybir.ActivationFunctionType
ALU = mybir.AluOpType

ENGS = ["tensor", "vector", "scalar", "gpsimd", "sync"]
_ESZ = {F32: 4, BF16: 2}


def _esz(dt):
    return _ESZ.get(dt, 4)


class _Op:
    __slots__ = ("fn", "cdeps", "ddeps", "sig", "dma", "pre")

    def __init__(self, fn):
        self.fn = fn
        self.cdeps = []
        self.ddeps = []
        self.sig = False
        self.dma = None
        self.pre = None


class Sched:
    GR = 256
    NDS = 8
    SAME_ENGINE_SYNC = True

    def __init__(self, nc):
        self.nc = nc
        self.ops = {e: [] for e in ENGS}
        self.seen = {e: {f: -1 for f in ENGS} for e in ENGS}
        self.seen_d = {e: {} for e in ENGS}
        self.lastw = {}
        self.readers = {}
        self.ndma = {e: 0 for e in ENGS}

    def _keys(self, ap):
        if isinstance(ap, tuple):
            return [ap]
        t = ap.tensor
        space = str(ap.space)
        if "DRAM" in space.upper() or "HBM" in space.upper():
            return [("d", t.name)]
        row = int(t.shape[1]) if len(t.shape) > 1 else 1
        for s in t.shape[2:]:
            row *= int(s)
        col = int(ap.offset) % row
        lo = col
        hi = col
        for (st, cnt) in list(ap.ap)[1:]:
            st = int(st); cnt = int(cnt)
            if st >= 0:
                hi += st * (cnt - 1)
            else:
                lo += st * (cnt - 1)
        esz = _esz(ap.dtype)
        if "PSUM" in space.upper():
            return [("p", t.name)]
        b0 = (lo * esz) // self.GR
        b1 = (hi * esz + esz - 1) // self.GR
        return [("s", t.name, g) for g in range(b0, b1 + 1)]

    def add(self, eng, fn, reads=(), writes=(), dma=False):
        op = _Op(fn)
        idx = len(self.ops[eng])
        toks = set()
        rkeys = []
        wkeys = []
        for a in reads:
            if a is None or isinstance(a, (int, float)):
                continue
            rkeys += self._keys(a)
        for a in writes:
            wkeys += self._keys(a)
        for k in rkeys:
            w = self.lastw.get(k)
            if w is not None:
                toks.add(w)
        for k in wkeys:
            w = self.lastw.get(k)
            if w is not None:
                toks.add(w)
            for r in self.readers.get(k, ()):
                toks.add(r)
        if dma:
            n = self.ndma[eng]
            slot = n % self.NDS
            val = 16 * (n // self.NDS + 1)
            self.ndma[eng] = n + 1
            op.dma = (slot, val)
            if n >= self.NDS:
                op.pre = (slot, val - 16)
            mytok = ("d", eng, slot, val)
        else:
            mytok = ("c", eng, idx)
        for tk in toks:
            if tk[0] == "c":
                _, f, k = tk
                if f == eng and (not self.SAME_ENGINE_SYNC or f == "tensor") and not dma:
                    continue
                if self.seen[eng][f] >= k:
                    continue
                self.seen[eng][f] = k
                self.ops[f][k].sig = True
                op.cdeps.append((f, k))
            else:
                _, q, slot, val = tk
                if self.seen_d[eng].get((q, slot), 0) >= val:
                    continue
                self.seen_d[eng][(q, slot)] = val
                op.ddeps.append((q, slot, val))
        for k in wkeys:
            self.lastw[k] = mytok
            self.readers[k] = []
        for k in rkeys:
            if k in wkeys:
                continue
            self.readers.setdefault(k, []).append(mytok)
        self.ops[eng].append(op)
        return mytok

    def final_wait(self, eng, toks):
        op = _Op(None)
        for tk in toks:
            if tk[0] == "c":
                _, f, k = tk
                self.ops[f][k].sig = True
                op.cdeps.append((f, k))
            else:
                op.ddeps.append((tk[1], tk[2], tk[3]))
        self.ops[eng].append(op)

    def emit(self):
        nc = self.nc
        import contextlib
        with contextlib.ExitStack() as es:
            csem = {e: es.enter_context(nc.semaphore("c_" + e)) for e in ENGS}
            dsem = {e: [es.enter_context(nc.semaphore("d_%s_%d" % (e, i))) for i in range(self.NDS)]
                    for e in ENGS if self.ndma[e] > 0}
            cval = {}
            for e in ENGS:
                c = 0
                vals = []
                for op in self.ops[e]:
                    if op.sig:
                        c += 1
                    vals.append(c)
                cval[e] = vals
            block = es.enter_context(nc.Block())

            def body(ename):
                def _f(eng):
                    for i, op in enumerate(self.ops[ename]):
                        if op.pre is not None:
                            eng.wait_ge(dsem[ename][op.pre[0]], op.pre[1])
                        for (f, k) in op.cdeps:
                            eng.wait_ge(csem[f], cval[f][k])
                        for (q, slot, val) in op.ddeps:
                            eng.wait_ge(dsem[q][slot], val)
                        if op.fn is None:
                            continue
                        ins = op.fn(eng)
                        if op.dma is not None:
                            ins.then_inc(dsem[ename][op.dma[0]], 16)
                        elif op.sig:
                            ins.then_inc(csem[ename], 1)
                return _f

            block.tensor(body("tensor"))
            block.vector(body("vector"))
            block.scalar(body("scalar"))
            block.gpsimd(body("gpsimd"))
            block.sync(body("sync"))

    def mm(self, out, lhsT, rhs, start=True, stop=True, tp=None):
        kw = {}
        if tp is not None:
            kw["tile_position"] = tp
        return self.add("tensor", lambda e: e.matmul(out, lhsT, rhs, start=start, stop=stop, skip_group_check=True, **kw),
                        reads=[lhsT, rhs] + ([] if start else [out]), writes=[out])

    def tr(self, out, in_, ident):
        return self.add("tensor", lambda e: e.transpose(out, in_, ident), reads=[in_, ident], writes=[out])

    def act(self, out, in_, func, bias=None, scale=None, eng="scalar"):
        kw = {}
        if bias is not None:
            kw["bias"] = bias
        if scale is not None:
            kw["scale"] = scale
        return self.add("scalar", lambda e: e.activation(out, in_, func, **kw),
                        reads=[in_, bias, scale], writes=[out])

    def tt(self, out, in0, in1, op, eng="vector"):
        return self.add(eng, lambda e: e.tensor_tensor(out, in0, in1, op), reads=[in0, in1], writes=[out])

    def ts(self, out, in0, s1, s2, op0, op1=None, eng="vector"):
        if op1 is None:
            return self.add(eng, lambda e: e.tensor_scalar(out, in0, s1, None, op0), reads=[in0, s1], writes=[out])
        return self.add(eng, lambda e: e.tensor_scalar(out, in0, s1, s2, op0, op1), reads=[in0, s1, s2], writes=[out])

    def stt(self, out, in0, scalar, in1, op0, op1, eng="vector"):
        return self.add(eng, lambda e: e.scalar_tensor_tensor(out, in0, scalar, in1, op0, op1),
                        reads=[in0, scalar, in1], writes=[out])

    def scan(self, out, d0, d1, init, op0=ALU.mult, op1=ALU.add):
        return self.add("vector", lambda e: e.tensor_tensor_scan(out, d0, d1, init, op0, op1),
                        reads=[d0, d1, init], writes=[out])

    def copy(self, out, in_, eng="vector"):
        if eng == "scalar":
            return self.act(out, in_, AF.Copy)
        return self.add(eng, lambda e: e.tensor_copy(out, in_), reads=[in_], writes=[out])

    def memset(self, out, val, eng="vector"):
        return self.add(eng, lambda e: e.memset(out, val), reads=[], writes=[out])

    def recip(self, out, in_):
        return self.add("vector", lambda e: e.reciprocal(out, in_), reads=[in_], writes=[out])

    def dma(self, out, in_, q="sync", rkey=None, wkey=None, slow=False):
        kw = {"allow_slow_non_contiguous": True} if slow else {}
        return self.add(q, lambda e: e.dma_start(out, in_, **kw),
                        reads=(list(rkey) if rkey is not None else [in_]),
                        writes=(list(wkey) if wkey is not None else [out]), dma=True)

from concourse.bass_utils import run_bass_kernel_spmd

import contextlib

NCORES = 8
DM = 1024
DEPTH = 2
NIN = 5120
DFF = 2816
ALPHA = float((2 * DEPTH) ** 0.25)
LN_EPS = 1e-5
PI = float(np.pi)


def mk(base, dims, off=0):
    p = list(base.ap)[0]
    return bass.AP(base.tensor, int(base.offset) + int(off), [[int(p[0]), int(p[1])]] + [[int(s), int(c)] for (s, c) in dims])


class Arena:
    def __init__(self, ar, total_f32):
        self.ar = ar
        self.top = 0
        self.total = total_f32 * 4
        self.jump = None

    def alloc(self, nbytes, dt=F32, at=None):
        nbytes = (nbytes + 255) // 256 * 256
        if at is None:
            at = self.top
            if self.jump is not None and at < self.jump[0] and at + nbytes > self.jump[0]:
                at = self.jump[1]
            self.top = at + nbytes
            assert self.top <= self.total, ("arena overflow", self.top, self.total)
        a = self.ar[:, at // 4:(at + nbytes) // 4]
        if dt != F32:
            a = a.bitcast(dt)
        return a


def build(nc, dbg=False):
    S = Sched(nc)
    dram = {}

    def din(name, shape):
        dram[name] = nc.dram_tensor(name, list(shape), F32, kind="ExternalInput").ap()
        return dram[name]

    def dout(name, shape):
        dram[name] = nc.dram_tensor(name, list(shape), F32, kind="ExternalOutput").ap()
        return dram[name]

    xp = din("xp", [2048, DM]); xs = din("xs", [64, DM]); call = din("call", [17, DM])
    st_h = din("st_h", [DEPTH, 16, DM]); st_cv = din("st_cv", [DEPTH, 16, 3, DM])
    st_re = din("st_re", [DEPTH, 16, 64, 64]); st_im = din("st_im", [DEPTH, 16, 64, 64])
    w_ada = din("w_ada", [DEPTH, DM, 6 * DM]); b_ada = din("b_ada", [DEPTH, 6 * DM])
    w_in = din("w_in", [DEPTH, DM, NIN]); b_in = din("b_in", [DEPTH, NIN])
    w_conv = din("w_conv", [DEPTH, 4, DM]); b_conv = din("b_conv", [DEPTH, DM])
    w_gates = din("w_lru_gates", [DEPTH, 2, 16, 64, 64]); b_gates = din("b_lru_gates", [DEPTH, 2, DM])
    lam = din("lru_lambda", [DEPTH, DM]); w_lo = din("w_lru_out", [DEPTH, DM, DM])
    a_re = din("s5_a_re", [DEPTH, 64, 64]); a_im = din("s5_a_im", [DEPTH, 64, 64]); log_dt = din("s5_log_dt", [DEPTH, 64])
    b_re = din("s5_b_re", [DEPTH, 64, 64, 16]); b_im = din("s5_b_im", [DEPTH, 64, 64, 16])
    c_re = din("s5_c_re", [DEPTH, 64, 16, 64]); c_im = din("s5_c_im", [DEPTH, 64, 16, 64])
    s5_d = din("s5_d", [DEPTH, DM]); w_glu = din("w_s5_glu", [DEPTH, DM, 2 * DM]); w_out = din("w_out", [DEPTH, DM, DM])
    ln1_g = din("ln1_g", [DEPTH, DM]); ln1_b = din("ln1_b", [DEPTH, DM])
    w_up = din("w_ffn_up", [DEPTH, DM, 2 * DFF]); w_dn = din("w_ffn_down", [DEPTH, DFF, DM])
    ln2_g = din("ln2_g", [DEPTH, DM]); ln2_b = din("ln2_b", [DEPTH, DM])
    identd = din("ident", [128, 128]); maskd = din("mask8", [128, 8])

    yp = dout("yp", [2048, DM]); ys = dout("ys", [64, DM])
    o_hp = dout("o_hp", [DEPTH, DM]); o_hs = dout("o_hs", [DEPTH, 16, DM])
    o_cp = dout("o_cp", [DEPTH, 3, DM]); o_cs = dout("o_cs", [DEPTH, 16, 3, DM])
    o_rp = dout("o_rp", [DEPTH, 64, 64]); o_rs = dout("o_rs", [DEPTH, 16, 64, 64])
    o_ip = dout("o_ip", [DEPTH, 64, 64]); o_is = dout("o_is", [DEPTH, 16, 64, 64])
    scrR = [nc.dram_tensor("scrR%d" % l, [8, 128, 8 * 4 * 2 * 128], BF16).ap() for l in range(DEPTH)]
    scrL = [nc.dram_tensor("scrL%d" % l, [8, 128, 8 * 4 * 2 * 128], BF16).ap() for l in range(DEPTH)]
    scrK = [nc.dram_tensor("scrK%d" % l, [8, 128, 8 * 128], BF16).ap() for l in range(DEPTH)]
    fin = []

    es = contextlib.ExitStack()
    TOT = 51 * 1024
    ar = es.enter_context(nc.sbuf_tensor("arena", [128, TOT], F32))
    PS = [es.enter_context(nc.psum_tensor("psb%d" % i, [128, 512], F32)) for i in range(8)]
    A = Arena(ar, TOT)
    psn = [0]

    def ps():
        psn[0] += 1
        return PS[psn[0] % 8][:, :]

    ident = A.alloc(512); S.dma(ident, identd)
    mask8 = A.alloc(32); S.dma(mask8[:, 0:8], maskd)
    onesb = A.alloc(256, BF16)
    S.memset(onesb, 1.0 / 1024.0)

    def fm(vec):
        return vec.rearrange("(t p) -> p t", p=128)

    P = []
    for l in range(DEPTH):
        d = {}
        d["pb1"] = A.alloc(88 * 4)
        d["pb2"] = A.alloc(104 * 4)
        d["b_in"] = d["pb1"][:, 0:40]
        d["b_ada"] = d["pb1"][:, 40:88]
        for i_, nm in enumerate(("b_conv", "lam", "s5_d", "ln1_g", "ln1_b", "ln2_g", "ln2_b", "bg0", "bg1")):
            d[nm] = d["pb2"][:, 8 * i_:8 * i_ + 8]
        d["wcv"] = mk(d["pb2"][:, 72:73], [(1, 8), (8, 4)])
        d["ada"] = A.alloc(48 * 17 * 4)
        d["adav"] = d["ada"][:, 0:48 * 17].rearrange("p (o s) -> p o s", s=17)
        d["wg"] = A.alloc(2 * 8 * 128 * 2, BF16)
        d["wgv"] = d["wg"][:, 0:2048].rearrange("p (g t c) -> p g t c", g=2, t=8)
        S.memset(d["wg"], 0.0)
        for g in range(2):
            for t in range(8):
                for h2 in range(2):
                    S.dma(d["wgv"][64 * h2:64 * h2 + 64, g, t, 64 * h2:64 * h2 + 64], w_gates[l, g, 2 * t + h2], q="gpsimd")
        d["clam"] = A.alloc(32)
        d["hst"] = A.alloc(32); S.memset(d["hst"], 0.0)
        d["convc"] = A.alloc(128); S.memset(d["convc"], 0.0)
        d["xst"] = A.alloc(256); S.memset(d["xst"], 0.0)
        ablk = A.alloc(1024)
        d["A8q"] = ablk[:, 0:128]
        for i_, nm in enumerate(("A4r", "A4i", "A4n")):
            d[nm] = ablk[:, 128 + 32 * i_:160 + 32 * i_]
        P.append(d)

    NMAX = 512
    bigbase = A.top
    X = A.alloc(8 * NMAX * 4)
    U = A.alloc(8 * NMAX * 2, BF16)
    ZD = A.alloc(8 * NMAX * 2, BF16)
    MA = A.alloc(8 * NMAX * 2, BF16)
    MB = A.alloc(8 * NMAX * 2, BF16)
    MIXA = A.alloc(8 * NMAX * 2, BF16)
    wb_start = A.top
    WB = [A.alloc(8 * 512 * 2, BF16) for _ in range(2)]
    CB = [A.alloc(16384, BF16) for _ in range(2)]
    KB = [A.alloc(2048, BF16) for _ in range(2)]
    STG = [A.alloc(4096) for _ in range(2)]
    regtop = A.top
    EXT = A.alloc(8 * (NMAX + 3) * 4)
    HG = A.alloc(8 * NMAX * 2, BF16)
    TMP = [A.alloc(NMAX * 4) for _ in range(4)]
    TVB = A.alloc(NMAX * 2, BF16)
    SSB = A.alloc(64 * 64 * 2, BF16)
    XIN = A.alloc(65 * 64 * 4)
    XINB = A.alloc(64 * 64 * 2, BF16)
    HH = X[:, 3072:3328]
    AD = ZD
    LNT = [XIN[:, 512 * i:512 * i + 512] for i in range(3)]
    lru_top = A.top
    s5_top = A.top
    A.top = regtop
    HF = A.alloc(22 * NMAX * 2, BF16)
    assert A.top <= lru_top - 0 and 22 * NMAX * 2 <= (8 * (NMAX + 3) * 4 + 255) // 256 * 256 + 8 * NMAX * 2
    A.top = max(lru_top, s5_top, A.top)
    print("arena used KB", A.top / 1024.0)

    rows128 = lambda v: v.rearrange("(t p) -> t p", p=128)
    for l in range(DEPTH):
        d = P[l]
        s1 = STG[0][:, 0:128]
        S.dma(s1[0:40, :], rows128(b_in[l]))
        S.dma(s1[40:88, :], rows128(b_ada[l]))
        pp = ps()
        S.tr(pp[:, 0:88], s1[0:88, :], ident[0:88, 0:88])
        S.copy(d["pb1"][:, 0:88], pp[:, 0:88])
        s2 = STG[1][:, 0:128]
        for i_, src in enumerate((b_conv[l], lam[l], s5_d[l], ln1_g[l], ln1_b[l], ln2_g[l], ln2_b[l], b_gates[l, 0], b_gates[l, 1])):
            S.dma(s2[8 * i_:8 * i_ + 8, :], rows128(src))
        for k in range(4):
            S.dma(s2[72 + 8 * k:72 + 8 * k + 8, :], rows128(w_conv[l, k]))
        pp = ps()
        S.tr(pp[:, 0:104], s2[0:104, :], ident[0:104, 0:104])
        S.copy(d["pb2"][:, 0:104], pp[:, 0:104])

    wn = [0]
    on = [0]

    def odma(out, in_, slow=False):
        on[0] += 1
        return S.dma(out, in_, wkey=[("o", on[0])], slow=slow)

    wscr = {}

    def cast_in(buf, wsrc, K, colsets):
        tot = sum(c for _, c in colsets)
        assert K * tot * 2 <= 8192
        v = buf[:, 0:K * tot].rearrange("p (k c) -> p k c", k=K)
        src = wsrc.rearrange("(k p) n -> p k n", p=128)
        o = 0
        outs = []
        for (c0, ncl) in colsets:
            S.dma(v[:, :, o:o + ncl], src[:, :, c0:c0 + ncl], q="gpsimd")
            outs.append((v, o, ncl))
            o += ncl
        return outs

    def stream(wsrc, K, colsets, key=None):
        wn[0] += 1
        buf = WB[wn[0] % 2]
        if key is None:
            return cast_in(buf, wsrc, K, colsets)
        tot = sum(c for _, c in colsets)
        scr = wscr[key]
        S.dma(buf[:, 0:K * tot], scr, q="sync", rkey=[("wscr", key)])
        v = buf[:, 0:K * tot].rearrange("p (k c) -> p k c", k=K)
        outs = []
        o = 0
        for (c0, ncl) in colsets:
            outs.append((v, o, ncl))
            o += ncl
        return outs

    def wchunks(l):
        r = []
        for ch in range(10):
            r.append((("in", l, ch), w_in[l], 8, [(512 * ch, 512)]))
        for ch in range(2):
            r.append((("lo", l, ch), w_lo[l], 8, [(512 * ch, 512)]))
        for ch in range(4):
            r.append((("glu", l, ch), w_glu[l], 8, [(256 * ch, 256), (1024 + 256 * ch, 256)]))
        for ch in range(2):
            r.append((("out", l, ch), w_out[l], 8, [(512 * ch, 512)]))
        for ch in range(11):
            r.append((("up", l, ch), w_up[l], 8, [(256 * ch, 256), (DFF + 256 * ch, 256)]))
        for ot in range(8):
            r.append((("dn", l, ot), w_dn[l], 22, [(128 * ot, 128)]))
        return r

    def convert_weights():
        bufs = [WB[0], WB[1], CB[0][:, 0:4096], CB[0][:, 4096:8192], CB[1][:, 0:4096], CB[1][:, 4096:8192]]
        allc = wchunks(0) + wchunks(1)
        LAG = 3
        pend = []
        for i, (key, wsrc, K, colsets) in enumerate(allc):
            tot = sum(c for _, c in colsets)
            wscr[key] = nc.dram_tensor("wb_%s_%d_%d" % key, [128, K * tot], BF16).ap()
            buf = bufs[i % len(bufs)]
            cast_in(buf, wsrc, K, colsets)
            pend.append((key, buf, K * tot))
            if len(pend) > LAG:
                k2, b2, n2 = pend.pop(0)
                S.dma(wscr[k2], b2[:, 0:n2], q="gpsimd", wkey=[("wscr", k2)])
        for (k2, b2, n2) in pend:
            S.dma(wscr[k2], b2[:, 0:n2], q="gpsimd", wkey=[("wscr", k2)])

    def lhs(wv, kt, j):
        v, o, ncl = wv
        return v[:, kt, o + 128 * j:o + 128 * j + 128]

    def finish():
        S.final_wait("sync", fin)
        S.emit()
        es.close()
        return nc
    if STOP == 1:
        return finish()
    if not SKIP_ADA:
      CT = TMP[0][:, 0:8 * 17].rearrange("p (k s) -> p k s", s=17)
      S.dma(STG[0][0:17, :], call)
      ppc = ps()
      for k in range(8):
          S.tr(ppc[:, 32 * k:32 * k + 17], STG[0][0:17, 128 * k:128 * k + 128], ident[0:17, 0:17])
      S.copy(CT, mk(ppc[:, 0:1], [(32, 8), (1, 17)]))
      SCB = TVB[:, 0:8 * 17].rearrange("p (k s) -> p k s", s=17)
      S.act(SCB, CT, AF.Silu)
      for l in range(DEPTH):
          for ch in range(12):
              (wv,) = stream(w_ada[l], 8, [(512 * ch, 512)])
              pp = ps()
              for j in range(4):
                  ot = 4 * ch + j
                  for kt in range(8):
                      S.mm(pp[:, 32 * j:32 * j + 17], lhs(wv, kt, j), SCB[:, kt, :], start=(kt == 0), stop=(kt == 7))
              for j in range(4):
                  ot = 4 * ch + j
                  S.act(P[l]["adav"][:, ot, :], pp[:, 32 * j:32 * j + 17], AF.Identity, bias=P[l]["b_ada"][:, ot:ot + 1])
          for idx in (1, 4):
              v = P[l]["adav"][:, 8 * idx:8 * idx + 8, :]
              S.ts(v, v, 1.0, None, ALU.add)
          t0 = TMP[1][:, 0:8]
          S.act(t0, P[l]["lam"][:, 0:8], AF.Exp, scale=-1.0)
          S.act(t0, t0, AF.Ln, bias=1.0)
          S.ts(P[l]["clam"][:, 0:8], t0, -8.0, None, ALU.mult)

    def s5_pre(l):
        d = P[l]
        A.top = bigbase
        A.jump = (wb_start, regtop)

        BIG = [A.alloc(16384) for _ in range(3)]
        SMALL = A.alloc(18 * 128)
        qn = [0]

        def q32():
            qn[0] += 1
            return SMALL[:, 32 * (qn[0] - 1):32 * qn[0]]
        AR, AI, DT, ADR, ADI, MAG, CNT, Y1, SN, CS, ABR, ABI, NR, DEN, FR, FI, T_a, T_b = [q32() for _ in range(18)]
        ast = STG[0][:, 0:128]
        for t in range(8):
            for io, srcd in enumerate((a_re, a_im)):
                S.dma(ast[32 * io + 4 * t:32 * io + 4 * t + 4, :].rearrange("p (g q) -> p g q", q=16),
                      bass.AP(srcd.tensor, int(srcd[l, 8 * t, 0:1].offset), [[16, 4], [64, 8], [1, 16]]))
        ppa = ps()
        S.tr(ppa[:, 0:64], ast[0:64, :], ident[0:64, 0:64])
        S.copy(AR, ppa[:, 0:32])
        S.copy(AI, ppa[:, 32:64])
        for gl in range(8):
            src = bass.AP(log_dt.tensor, int(log_dt[l, gl:gl + 1].offset), [[0, 16], [8, 8]])
            S.dma(DT[16 * gl:16 * gl + 16, 0:8], src, slow=True)
        S.act(DT[:, 0:8], DT[:, 0:8], AF.Exp)
        dtb = mk(DT[:, 0:8], [(1, 8), (0, 4)])
        v3 = lambda a: a.rearrange("p (t h) -> p t h", h=4)
        S.tt(v3(ADR), v3(AR), dtb, ALU.mult)
        S.tt(v3(ADI), v3(AI), dtb, ALU.mult)
        S.act(MAG, ADR, AF.Exp)

        def sin_of(out, x, shift):
            S.ts(Y1, x, shift, None, ALU.add)
            S.ts(CNT, Y1, PI, None, ALU.is_gt)
            for m in (3, 5, 7, 9, 11, 13, 15):
                S.stt(CNT, Y1, m * PI, CNT, ALU.is_gt, ALU.add)
            S.stt(Y1, CNT, -2.0 * PI, Y1, ALU.mult, ALU.add)
            S.ts(Y1, Y1, PI, -PI, ALU.min, ALU.max)
            S.act(out, Y1, AF.Sin)
        sin_of(SN, ADI, 0.0)
        sin_of(CS, ADI, PI / 2)
        S.tt(ABR, MAG, CS, ALU.mult)
        S.tt(ABI, MAG, SN, ALU.mult)
        S.ts(NR, ABR, -1.0, None, ALU.add)
        S.tt(DEN, AR, AR, ALU.mult)
        S.tt(T_a, AI, AI, ALU.mult)
        S.tt(DEN, DEN, T_a, ALU.add)
        S.recip(DEN, DEN)
        S.tt(FR, NR, AR, ALU.mult); S.tt(T_a, ABI, AI, ALU.mult); S.tt(FR, FR, T_a, ALU.add); S.tt(FR, FR, DEN, ALU.mult)
        S.tt(FI, ABI, AR, ALU.mult); S.tt(T_a, NR, AI, ALU.mult); S.tt(FI, FI, T_a, ALU.subtract); S.tt(FI, FI, DEN, ALU.mult)
        PWR = A.alloc(9 * 128)[:, 0:9 * 32].rearrange("p (k x) -> p k x", x=32)
        PWI = A.alloc(9 * 128)[:, 0:9 * 32].rearrange("p (k x) -> p k x", x=32)
        S.memset(PWR[:, 0, :], 1.0); S.memset(PWI[:, 0, :], 0.0)
        for k in range(8):
            S.tt(T_a, PWR[:, k, :], ABR, ALU.mult); S.tt(T_b, PWI[:, k, :], ABI, ALU.mult)
            S.tt(PWR[:, k + 1, :], T_a, T_b, ALU.subtract)
            S.tt(T_a, PWR[:, k, :], ABI, ALU.mult); S.tt(T_b, PWI[:, k, :], ABR, ALU.mult)
            S.tt(PWI[:, k + 1, :], T_a, T_b, ALU.add)
        for (nr_, ni_, nn_, k) in (("A4r", "A4i", "A4n", 4),):
            S.copy(d[nr_][:, 0:32], PWR[:, k, :]); S.copy(d[ni_][:, 0:32], PWI[:, k, :])
            S.ts(d[nn_][:, 0:32], PWI[:, k, :], -1.0, None, ALU.mult)
        aq = d["A8q"][:, 0:128].rearrange("p (x f) -> p x f", f=4)
        S.copy(aq[:, :, 0], PWR[:, 8, :]); S.copy(aq[:, :, 3], PWR[:, 8, :])
        S.copy(aq[:, :, 2], PWI[:, 8, :]); S.ts(aq[:, :, 1], PWI[:, 8, :], -1.0, None, ALU.mult)
        def q512():
            return A.alloc(2048)[:, 0:512].rearrange("p (x q) -> p x q", q=16)
        BR, BI, CR, CI, BBR, BBI, T1 = [q512() for _ in range(7)]
        ER, EI, T2 = BR, BI, T1
        for gl in range(8):
            for (dst, srcd) in ((BR, b_re), (BI, b_im)):
                for t in range(8):
                    S.dma(dst[16 * gl:16 * gl + 16, 4 * t:4 * t + 4, :],
                          bass.AP(srcd.tensor, int(srcd[l, 8 * t + gl, 0, 0:1].offset), [[16, 16], [256, 4], [1, 16]]))
        CQ = BIG[0]
        for (dst, srcd) in ((CR, c_re), (CI, c_im)):
            ppq = ps()
            for t in range(8):
                for ph in range(4):
                    blk = CQ[0:16, (4 * t + ph) * 128:(4 * t + ph) * 128 + 128]
                    S.dma(blk.rearrange("q (g p) -> q g p", p=16),
                          bass.AP(srcd.tensor, int(srcd[l, 8 * t, 0, 16 * ph:16 * ph + 1].offset), [[64, 16], [1024, 8], [1, 16]]))
                    S.tr(ppq[:, (4 * t + ph) * 16:(4 * t + ph) * 16 + 16], blk, ident[0:16, 0:16])
            S.copy(dst, ppq[:, 0:512].rearrange("p (x q) -> p x q", q=16))
        b16 = lambda a: mk(a, [(1, 32), (0, 16)])
        S.tt(BBR, BR, b16(FR), ALU.mult); S.tt(T1, BI, b16(FI), ALU.mult); S.tt(BBR, BBR, T1, ALU.subtract)
        S.tt(BBI, BI, b16(FR), ALU.mult); S.tt(T1, BR, b16(FI), ALU.mult); S.tt(BBI, BBI, T1, ALU.add)
        def qbd(dt=F32):
            sz = 32 * 128 * (4 if dt == F32 else 2)
            return A.alloc(sz, dt)[:, 0:4096].rearrange("p (x g q) -> p x g q", g=8, q=16)
        EBR0 = CQ[:, 0:4096].rearrange("p (x g q) -> p x g q", g=8, q=16)
        v4 = lambda a: a[:, 0:4096].rearrange("p (x g q) -> p x g q", g=8, q=16)
        EBI0, CBR = v4(BIG[1]), v4(BIG[2])
        CBN, EBR1, EBI1 = qbd(), qbd(), qbd()
        EBS = [(EBR0, EBI0), (EBR1, EBI1)]
        LBDS = [mk(EBI0[:, 0, 0, 0:1], [(1, 4096)]).bitcast(BF16)[:, 0:4096].rearrange("p (x g q) -> p x g q", g=8, q=16),
                mk(EBI1[:, 0, 0, 0:1], [(1, 4096)]).bitcast(BF16)[:, 0:4096].rearrange("p (x g q) -> p x g q", g=8, q=16)]
        RSTS = [A.alloc(4 * 128 * 2 * 2, BF16) for _ in range(2)]
        KSTS = [A.alloc(128 * 2, BF16) for _ in range(2)]
        DD = A.alloc(512)
        print("s5_pre arena top KB", A.top / 1024.0)
        mexp = mk(mask8[:, 0:8], [(0, 32), (1, 8), (0, 16)])
        ex = lambda a: mk(a, [(16, 32), (0, 8), (1, 16)])
        S.tt(CBR, ex(CR), mexp, ALU.mult)
        S.ts(T2, CI, -1.0, None, ALU.mult)
        S.tt(CBN, ex(T2), mexp, ALU.mult)
        def e_stage(s):
            k = 7 - s
            pr, pi_ = b16(PWR[:, k, :]), b16(PWI[:, k, :])
            S.tt(ER, BBR, pr, ALU.mult); S.tt(T1, BBI, pi_, ALU.mult); S.tt(ER, ER, T1, ALU.subtract)
            S.tt(EI, BBI, pr, ALU.mult); S.tt(T1, BBR, pi_, ALU.mult); S.tt(EI, EI, T1, ALU.add)
            EBR, EBI = EBS[s % 2]
            S.tt(EBR, ex(ER), mexp, ALU.mult)
            S.tt(EBI, ex(EI), mexp, ALU.mult)

        e_stage(0)
        for s in range(8):
            k = 7 - s
            if s + 1 < 8:
                e_stage(s + 1)
            EBR, EBI = EBS[s % 2]
            for t in range(8):
                pp = ps()
                pp2 = ps()
                for ph in range(4):
                    for ri, EB in enumerate((EBR, EBI)):
                        tgt = (pp if ph < 2 else pp2)[:, ((ph % 2) * 2 + ri) * 128:((ph % 2) * 2 + ri) * 128 + 128]
                        S.tr(tgt, EB[:, 4 * t + ph].rearrange("p g q -> p (g q)"), ident)
                rv = RSTS[t % 2][:, 0:1024]
                S.act(rv[:, 0:512], pp, AF.Copy)
                S.copy(rv[:, 512:1024], pp2)
                dst = scrR[l][t].rearrange("p (s x) -> p s x", s=8)[:, s, :]
                S.dma(dst, rv, wkey=[("scrR", l, t, s)])
                pk = ps()
                n = 0
                for ph in range(4):
                    for (EB, CBx) in ((EBR, CBR), (EBI, CBN)):
                        S.mm(pk[:, 0:128], EB[:, 4 * t + ph].rearrange("p g q -> p (g q)"),
                             CBx[:, 4 * t + ph].rearrange("p g q -> p (g q)"), start=(n == 0), stop=(n == 7))
                        n += 1
                kv = KSTS[t % 2][:, 0:128]
                if k == 0:
                    S.ts(DD[:, 0:128], ident, d["s5_d"][:, t:t + 1], None, ALU.mult)
                    S.tt(kv, pk[:, 0:128], DD[:, 0:128], ALU.add)
                else:
                    S.copy(kv, pk[:, 0:128])
                dstk = scrK[l][t].rearrange("p (s x) -> p s x", s=8)[:, k, :]
                S.dma(dstk, kv, wkey=[("scrK", l, t, k)])
        for tt_ in range(8):
            pr, pi_ = b16(PWR[:, tt_ + 1, :]), b16(PWI[:, tt_ + 1, :])
            S.tt(ER, CR, pr, ALU.mult); S.tt(T1, CI, pi_, ALU.mult); S.tt(ER, ER, T1, ALU.subtract)
            S.tt(EI, CI, pr, ALU.mult); S.tt(T1, CR, pi_, ALU.mult); S.tt(EI, EI, T1, ALU.add)
            S.ts(EI, EI, -1.0, None, ALU.mult)
            for ri, Ex in enumerate((ER, EI)):
                LBD = LBDS[ri]
                S.tt(LBD, ex(Ex), mexp, ALU.mult)
                for t in range(8):
                    dst = scrL[l][t].rearrange("p (s h r x) -> p s h r x", s=8, h=4, r=2)[:, tt_, :, ri, :]
                    S.dma(dst, LBD[:, 4 * t:4 * t + 4].rearrange("p h g q -> p h (g q)"), wkey=[("scrL", l, t, tt_, ri)])
        A.top = max(lru_top, s5_top)
        A.jump = None

    if STOP == 2:
        return finish()
    convert_weights()
    for l in range(DEPTH):
        if not SKIP_S5PRE:
            s5_pre(l)
    if STOP == 3:
        return finish()

    def blockrun(bi):
        sample = (bi == 4)
        N = 64 if sample else 512
        L = 4 if sample else 8
        NC_ = 16 if sample else 64
        nseq = 16 if sample else 1
        xsrc = xs if sample else xp[512 * bi:512 * bi + 512]
        ydst = ys if sample else yp[512 * bi:512 * bi + 512]
        Xv = X[:, 0:8 * N].rearrange("p (k n) -> p k n", n=N)
        Uv = U[:, 0:8 * N].rearrange("p (k n) -> p k n", n=N)
        MIXv = Uv
        ZDv = ZD[:, 0:8 * N].rearrange("p (k n) -> p k n", n=N)
        XBv = ZDv
        MAv = MA[:, 0:8 * N].rearrange("p (k n) -> p k n", n=N)
        X2Bv = MAv
        MBv = MB[:, 0:8 * N].rearrange("p (k n) -> p k n", n=N)
        MIXAv = MIXA[:, 0:8 * N].rearrange("p (k n) -> p k n", n=N)
        EW = N + 3 * nseq
        EXTv = EXT[:, 0:8 * EW].rearrange("p (k n) -> p k n", n=EW)
        HGv = HG[:, 0:8 * N].rearrange("p (k n) -> p k n", n=N)
        ADv = AD[:, 0:8 * N].rearrange("p (k n) -> p k n", n=N)
        HFv = HF[:, 0:22 * N].rearrange("p (k n) -> p k n", n=N)
        T = [t[:, 0:N] for t in TMP]
        TVBv = TVB[:, 0:N]
        LN_ = [t[:, 0:N] for t in LNT]

        def nat_of_tm(ap2d, Aa, Bb):
            return mk(ap2d, [(1, Aa), (Aa, Bb)])
        def lru_nat(ap2d):
            return nat_of_tm(ap2d, 16, 4) if sample else ap2d
        def s5_nat(ap2d):
            return nat_of_tm(ap2d, NC_, L)
        def nat3(ap2d, Aa, Bb):
            return mk(ap2d, [(Bb, Aa), (1, Bb)])

        for g in range(N // 128 if not sample else 1):
            rows = 64 if sample else 128
            st = STG[g % 2]
            S.dma(st[0:rows, :], xsrc[128 * g:128 * g + rows, :])
            for half in range(2):
                pp = ps()
                for j in range(4):
                    k = 4 * half + j
                    S.tr(pp[:, 128 * j:128 * j + rows], st[0:rows, 128 * k:128 * k + 128], ident[0:rows, 0:rows])
                for j in range(4):
                    k = 4 * half + j
                    S.copy(Xv[:, k, 128 * g:128 * g + rows], pp[:, 128 * j:128 * j + rows], eng=("scalar" if half else "vector"))

        def modulate(out3, in3, l, isc, ish):
            ada = P[l]["adav"]
            for k in range(8):
                if not sample:
                    S.ts(out3[:, k, :], in3[:, k, :], ada[:, 8 * isc + k, 0:1], ada[:, 8 * ish + k, 0:1], ALU.mult, ALU.add)
                else:
                    sc = mk(ada[:, 8 * isc + k, 1:17], [(1, 16), (0, 4)])
                    sh = mk(ada[:, 8 * ish + k, 1:17], [(1, 16), (0, 4)])
                    S.tt(T[0], in3[:, k, :], sc, ALU.mult)
                    S.tt(out3[:, k, :], T[0], sh, ALU.add)

        def gate_resid(pp, l, ig, k):
            ada = P[l]["adav"]
            if not sample:
                S.ts(T[0], pp[:, 0:N], ada[:, 8 * ig + k, 0:1], None, ALU.mult)
            else:
                S.tt(T[0], pp[:, 0:N], mk(ada[:, 8 * ig + k, 1:17], [(1, 16), (0, 4)]), ALU.mult)
            S.stt(Xv[:, k, :], Xv[:, k, :], ALPHA, T[0], ALU.mult, ALU.add)

        def layernorm(l, gname, bname):
            S.copy(XBv, Xv)
            S.act(X2Bv, Xv, AF.Square)
            pm = ps(); pq = ps()
            for k in range(8):
                S.mm(pm[:, 0:N], onesb[:, 0:128], XBv[:, k, :], start=(k == 0), stop=(k == 7))
            for k in range(8):
                S.mm(pq[:, 0:N], onesb[:, 0:128], X2Bv[:, k, :], start=(k == 0), stop=(k == 7))
            M_, V_, R_ = LN_
            S.copy(M_, pm[:, 0:N])
            S.tt(V_, M_, M_, ALU.mult)
            S.tt(V_, pq[:, 0:N], V_, ALU.subtract)
            S.ts(V_, V_, LN_EPS, None, ALU.add)
            S.act(V_, V_, AF.Sqrt)
            S.recip(R_, V_)
            for k in range(8):
                Tk = T[k % 2]
                S.tt(Tk, Xv[:, k, :], M_, ALU.subtract)
                S.tt(Tk, Tk, R_, ALU.mult)
                S.act(Xv[:, k, :], Tk, AF.Identity, scale=P[l][gname][:, k:k + 1], bias=P[l][bname][:, k:k + 1])

        for l in range(DEPTH):
            d = P[l]
            if STOP == 10:
                raise StopIteration()
            modulate(Uv, Xv, l, 1, 0)
            if STOP == 11:
                raise StopIteration()
            if sample:
                for kk in range(3):
                    S.dma(STG[0][16 * kk:16 * kk + 16, :], st_cv[l, :, kk, :])
                ppc = ps()
                for k in range(8):
                    S.tr(ppc[:, 48 * k:48 * k + 48], STG[0][0:48, 128 * k:128 * k + 128], ident[0:48, 0:48])
                S.copy(mk(EXTv[:, 0, 0:1], [(EW, 8), (1, 48)]), mk(ppc[:, 0:1], [(48, 8), (1, 48)]))
            else:
                S.copy(EXTv[:, :, 0:3], d["convc"][:, 0:24].rearrange("p (k c) -> p k c", c=3))
            if STOP == 12:
                raise StopIteration()
            SSBv = SSB[:, 0:NC_ * 64].rearrange("p (c x) -> p c x", x=64)
            XINv = XIN[:, 0:65 * 64].rearrange("p (c x) -> p c x", x=64)
            cn = [0]
            XBf = XINB[:, 0:64 * NC_].rearrange("p (x c) -> p x c", c=NC_)

            def s_phase(k):
                cn[0] += 1
                cb = CB[cn[0] % 2]
                if sample:
                    S.dma(cb[:, 4096:8192], scrR[l][k][:, 4096:8192], rkey=[("scrR", l, k, s_) for s_ in range(4, 8)])
                else:
                    S.dma(cb[:, 0:8192], scrR[l][k], rkey=[("scrR", l, k, s_) for s_ in range(8)])
                Rv = cb[:, 0:8192].rearrange("p (s h r x) -> p s h r x", s=8, h=4, r=2)
                sp = ps()
                for ph in range(4):
                    for ri in range(2):
                        o = (2 * ph + ri) * NC_
                        for s in range(L):
                            S.mm(sp[:, o:o + NC_], Rv[:, s + 8 - L, ph, ri, :], ZDv[:, k, s * NC_:(s + 1) * NC_],
                                 start=(s == 0), stop=(s == L - 1))
                S.act(mk(SSBv[:, 0, 8 * k:8 * k + 8], [(1, 8), (64, NC_)]), sp[:, 0:8 * NC_].rearrange("p (x c) -> p x c", c=NC_), AF.Copy)

            def scan_all():
                Ar, Ai, An = (d["A4r"], d["A4i"], d["A4n"])
                if not sample:
                    S.copy(XINv[:, 0, :], d["xst"][:, 0:64], eng="gpsimd")
                    p4 = STG[0][:, 0:128].rearrange("p (x a b) -> p x a b", a=2, b=2)
                    t2 = STG[1][:, 0:64].rearrange("p (x r) -> p x r", r=2)
                    aq = d["A8q"][:, 0:128].rearrange("p (x a b) -> p x a b", a=2, b=2)
                    for c in range(NC_):
                        xrep = mk(XINv[:, c, 0:1], [(2, 32), (0, 2), (1, 2)])
                        S.tt(p4, xrep, aq, ALU.mult, eng="gpsimd")
                        S.tt(t2, p4[:, :, :, 0], p4[:, :, :, 1], ALU.add, eng="gpsimd")
                        S.tt(XINv[:, c + 1, :], STG[1][:, 0:64], SSBv[:, c, :], ALU.add, eng="gpsimd")
                    S.copy(d["xst"][:, 0:64], XINv[:, NC_, :], eng="gpsimd")
                    if bi == 3:
                        xr = STG[1][:, 0:64]
                        S.copy(xr.rearrange("p (r x) -> p r x", r=2), mk(d["xst"][:, 0:1], [(1, 2), (2, 32)]))
                        ppx = ps()
                        S.tr(ppx[0:64, 0:128], xr, ident)
                        S.copy(STG[0][0:64, 0:128], ppx[0:64, 0:128])
                        for ri, od in enumerate((o_rp, o_ip)):
                            for t in range(8):
                                r0 = 32 * ri + 4 * t
                                fin.append(odma(bass.AP(od.tensor, int(od[l, 8 * t, 0:1].offset), [[16, 4], [64, 8], [1, 16]]),
                                                STG[0][r0:r0 + 4, 0:128].rearrange("p (g q) -> p g q", q=16)))
                else:
                    SS = X[:, 1024:3072]
                    for ri, sd in enumerate((st_re, st_im)):
                        for kh in range(2):
                            ppq = ps()
                            for tl in range(4):
                                for ph in range(4):
                                    bl = 4 * tl + ph
                                    blk = SS[0:16, 128 * bl:128 * bl + 128]
                                    S.dma(blk.rearrange("s (g p) -> s g p", p=16),
                                          bass.AP(sd.tensor, int(sd[l, 0, 8 * (4 * kh + tl), 16 * ph:16 * ph + 1].offset), [[4096, 16], [64, 8], [1, 16]]))
                                    S.tr(ppq[:, 16 * bl:16 * bl + 16], blk, ident[0:16, 0:16])
                            S.copy(mk(XINv[:, 0, ri:ri + 1], [(2, 16), (64, 16)], off=32 * kh), mk(ppq[:, 0:1], [(16, 16), (1, 16)]))
                    x0 = XINv[:, 0:16, :].rearrange("p s (x r) -> p s x r", r=2)
                    x1 = XINv[:, 16:32, :].rearrange("p s (x r) -> p s x r", r=2)
                    ta = mk(STG[0][:, 0:1], [(64, 16), (2, 32), (1, 2)])
                    tb = mk(STG[1][:, 0:1], [(64, 16), (2, 32), (1, 2)])
                    bc = lambda a: mk(a[:, 0:32], [(0, 16), (1, 32)])
                    S.tt(ta, x0, mk(Ar[:, 0:32], [(0, 16), (1, 32), (0, 2)]), ALU.mult)
                    S.tt(tb[:, :, :, 0], x0[:, :, :, 1], bc(An), ALU.mult)
                    S.tt(tb[:, :, :, 1], x0[:, :, :, 0], bc(Ai), ALU.mult)
                    S.tt(ta, ta, tb, ALU.add)
                    S.tt(x1, ta, SSBv.rearrange("p c (x r) -> p c x r", r=2), ALU.add)
                    for ri, od in enumerate((o_rs, o_is)):
                        for kh in range(2):
                            for tl in range(4):
                                ppq = ps()
                                for ph in range(4):
                                    cidx = ((4 * kh + tl) * 4 + ph) * 2 + ri
                                    S.tr(ppq[0:16, 128 * ph:128 * ph + 128], mk(XINv[:, 16, cidx:cidx + 1], [(64, 16)]), ident)
                                S.copy(SS[0:16, 512 * tl:512 * tl + 512], ppq[0:16, :], eng=("scalar" if tl % 2 else "vector"))
                                for ph in range(4):
                                    bl = 4 * tl + ph
                                    fin.append(odma(bass.AP(od.tensor, int(od[l, 0, 8 * (4 * kh + tl), 16 * ph:16 * ph + 1].offset), [[4096, 16], [64, 8], [1, 16]]),
                                                    SS[0:16, 128 * bl:128 * bl + 128].rearrange("s (g p) -> s g p", p=16)))
                XBf = XINB[:, 0:64 * NC_].rearrange("p (x c) -> p x c", c=NC_)
                S.copy(XBf, mk(XINv[:, 0, :], [(1, 64), (64, NC_)]), eng="gpsimd")

            def y_phase(k):
                cn[0] += 1
                cb = CB[cn[0] % 2]
                kb = KB[cn[0] % 2]
                if sample:
                    S.dma(cb[:, 0:4096], scrL[l][k][:, 0:4096], rkey=[("scrL", l, k, a_, b_) for a_ in range(4) for b_ in range(2)])
                    S.dma(kb[:, 0:512], scrK[l][k][:, 0:512], rkey=[("scrK", l, k, s_) for s_ in range(4)])
                else:
                    S.dma(cb[:, 0:8192], scrL[l][k], rkey=[("scrL", l, k, a_, b_) for a_ in range(8) for b_ in range(2)])
                    S.dma(kb[:, 0:1024], scrK[l][k], rkey=[("scrK", l, k, s_) for s_ in range(8)])
                Lv = cb[:, 0:8192].rearrange("p (s h r x) -> p s h r x", s=8, h=4, r=2)
                Kv = kb[:, 0:1024].rearrange("p (s x) -> p s x", s=8)
                yp_ = ps()
                for t in range(L):
                    o = t * NC_
                    for s in range(t + 1):
                        S.mm(yp_[:, o:o + NC_], Kv[:, t - s, :], ZDv[:, k, s * NC_:(s + 1) * NC_], start=(s == 0), stop=False)
                    for ph in range(4):
                        for ri in range(2):
                            S.mm(yp_[:, o:o + NC_], Lv[:, t, ph, ri, :], XBf[:, 8 * k + 2 * ph + ri, :], start=False,
                                 stop=(ph == 3 and ri == 1))
                S.act(ADv[:, k, :], yp_[:, 0:N], AF.Gelu_apprx_tanh)

            def inproj_chunk(ch):
                (wv,) = stream(w_in[l], 8, [(512 * ch, 512)], key=("in", l, ch))
                for j in range(4):
                    ot = 4 * ch + j
                    pp = ps()
                    for kt in range(8):
                        S.mm(pp[:, 0:N], lhs(wv, kt, j), Uv[:, kt, :], start=(kt == 0), stop=(kt == 7))
                    bias = d["b_in"][:, ot:ot + 1]
                    k = ot % 8
                    src = pp[:, 0:N]
                    if ot < 8:
                        dst = EXTv[:, k, 3 * nseq:3 * nseq + N]
                        S.act(lru_nat(dst) if sample else dst, nat3(src, 16, 4) if sample else src, AF.Identity, bias=bias)
                    elif ot < 16:
                        dst = HGv[:, k, :]
                        S.act(lru_nat(dst) if sample else dst, nat3(src, 16, 4) if sample else src, AF.Gelu_apprx_tanh, bias=bias)
                    elif ot < 24:
                        S.act(s5_nat(ZDv[:, k, :]), nat3(src, NC_, L), AF.Identity, bias=bias)
                    elif ot < 32:
                        S.act(MAv[:, k, :], src, AF.Sigmoid, bias=bias)
                    else:
                        S.act(MBv[:, k, :], src, AF.Sigmoid, bias=bias)
            inproj_chunk(4)
            inproj_chunk(5)
            for i_, ch in enumerate((0, 2, 1, 3)):
                s_phase(i_)
                inproj_chunk(ch)
            if STOP == 4:
                raise StopIteration()
            if sample:
                for half in range(2):
                    ppc = ps()
                    for j in range(4):
                        k = 4 * half + j
                        S.tr(ppc[0:48, 128 * j:128 * j + 128], EXTv[:, k, 64:112], ident)
                    S.copy(STG[1][0:48, 512 * half:512 * half + 512], ppc[0:48, :])
                for kk in range(3):
                    fin.append(odma(o_cs[l, :, kk, :], STG[1][16 * kk:16 * kk + 16, :]))
            else:
                S.copy(d["convc"][:, 0:24].rearrange("p (k c) -> p k c", c=3), EXTv[:, :, N:N + 3])
                if bi == 3:
                    ppc = ps()
                    for kk in range(3):
                        S.tr(ppc[0:8, 128 * kk:128 * kk + 128], mk(d["convc"][:, kk:kk + 1], [(3, 8)]), ident)
                    S.copy(STG[1][0:8, 0:384], ppc[0:8, 0:384])
                    for kk in range(3):
                        fin.append(odma(rows128(o_cp[l, kk]), STG[1][0:8, 128 * kk:128 * kk + 128]))
            if sample:
                H0 = HH[:, 0:128].rearrange("p (k s) -> p k s", s=16)
                S.dma(STG[0][0:16, :], st_h[l])
                pph = ps()
                for k in range(8):
                    S.tr(pph[:, 16 * k:16 * k + 16], STG[0][0:16, 128 * k:128 * k + 128], ident[0:16, 0:16])
                S.copy(H0, pph[:, 0:128].rearrange("p (k s) -> p k s", s=16))
                HOUT = HH[:, 128:256].rearrange("p (k s) -> p k s", s=16)
            TS_ = [(T, TVBv), (T, TVBv)]

            def lru_s1(k):
                Tk, TVk = TS_[k % 2]
                wc = d["wcv"]
                S.ts(Tk[0], EXTv[:, k, 0:N], wc[:, k, 0:1], d["b_conv"][:, k:k + 1], ALU.mult, ALU.add)
                for kk in range(1, 4):
                    S.stt(Tk[0], EXTv[:, k, kk * nseq:kk * nseq + N], wc[:, k, kk:kk + 1], Tk[0], ALU.mult, ALU.add)
                S.act(TVk, Tk[0], AF.Copy)
                pr = ps(); pi_ = ps()
                S.mm(pr[:, 0:N], d["wgv"][:, 0, k, :], TVk)
                S.mm(pi_[:, 0:N], d["wgv"][:, 1, k, :], TVk)
                S.act(Tk[1], pr[:, 0:N], AF.Sigmoid, bias=d["bg0"][:, k:k + 1])
                S.act(Tk[2], pi_[:, 0:N], AF.Sigmoid, bias=d["bg1"][:, k:k + 1])
                S.act(Tk[3], Tk[1], AF.Exp, scale=d["clam"][:, k:k + 1])

            def lru_s2(k):
                Tk, TVk = TS_[k % 2]
                S.tt(Tk[1], Tk[3], Tk[3], ALU.mult)
                S.act(Tk[1], Tk[1], AF.Sqrt, scale=-1.0, bias=1.0)
                S.tt(Tk[2], Tk[2], Tk[0], ALU.mult)
                S.tt(Tk[2], Tk[2], Tk[1], ALU.mult)
                if not sample:
                    S.scan(Tk[1], Tk[3], Tk[2], d["hst"][:, k:k + 1])
                    S.copy(d["hst"][:, k:k + 1], Tk[1][:, N - 1:N])
                else:
                    for t in range(4):
                        prev = H0[:, k, :] if t == 0 else Tk[1][:, 16 * (t - 1):16 * t]
                        S.tt(Tk[1][:, 16 * t:16 * t + 16], Tk[3][:, 16 * t:16 * t + 16], prev, ALU.mult)
                        S.tt(Tk[1][:, 16 * t:16 * t + 16], Tk[1][:, 16 * t:16 * t + 16], Tk[2][:, 16 * t:16 * t + 16], ALU.add)
                    S.copy(HOUT[:, k, :], Tk[1][:, 48:64])
                S.tt(HGv[:, k, :], Tk[1], HGv[:, k, :], ALU.mult)

            for k in range(8):
                lru_s1(k)
                lru_s2(k)
                if k < 2:
                    s_phase(2 * k + 4)
                    s_phase(2 * k + 5)
                if 3 <= k <= 6:
                    inproj_chunk(k + 3)
                if k == 1:
                    scan_all()
            if sample:
                for half in range(2):
                    pph = ps()
                    for j in range(4):
                        S.tr(pph[0:16, 128 * j:128 * j + 128], HOUT[:, 4 * half + j, :], ident)
                    S.copy(STG[0][0:16, 512 * half:512 * half + 512], pph[0:16, :])
                fin.append(odma(o_hs[l], STG[0][0:16, :]))
            elif bi == 3:
                pph = ps()
                S.tr(pph[0:8, 0:128], d["hst"][:, 0:8], ident)
                S.copy(STG[0][0:8, 0:128], pph[0:8, 0:128])
                fin.append(odma(rows128(o_hp[l]), STG[0][0:8, 0:128]))
            if STOP == 5:
                raise StopIteration()
            for ch in range(2):
                (wv,) = stream(w_lo[l], 8, [(512 * ch, 512)], key=("lo", l, ch))
                for j in range(4):
                    ot = 4 * ch + j
                    pp = ps()
                    for kt in range(8):
                        S.mm(pp[:, 0:N], lhs(wv, kt, j), HGv[:, kt, :], start=(kt == 0), stop=(kt == 7))
                    if sample:
                        S.tt(nat3(MIXAv[:, ot, :], 16, 4), lru_nat(pp[:, 0:N]), nat3(MAv[:, ot, :], 16, 4), ALU.mult)
                    else:
                        S.tt(MIXAv[:, ot, :], pp[:, 0:N], MAv[:, ot, :], ALU.mult)
            if STOP == 6:
                raise StopIteration()
            for k in range(8):
                y_phase(k)
            if STOP == 7:
                raise StopIteration()
            for ch in range(4):
                wv, wg_ = stream(w_glu[l], 8, [(256 * ch, 256), (1024 + 256 * ch, 256)], key=("glu", l, ch))
                for j in range(2):
                    ot = 2 * ch + j
                    pv = ps(); pg = ps()
                    for kt in range(8):
                        S.mm(pv[:, 0:N], lhs(wv, kt, j), ADv[:, kt, :], start=(kt == 0), stop=(kt == 7))
                    for kt in range(8):
                        S.mm(pg[:, 0:N], lhs(wg_, kt, j), ADv[:, kt, :], start=(kt == 0), stop=(kt == 7))
                    S.act(T[1], pg[:, 0:N], AF.Sigmoid)
                    S.tt(T[2], pv[:, 0:N], T[1], ALU.mult)
                    S.tt(nat3(T[3], NC_, L), s5_nat(T[2]), nat3(MBv[:, ot, :], NC_, L), ALU.mult)
                    S.tt(MIXv[:, ot, :], T[3], MIXAv[:, ot, :], ALU.add)
            if STOP == 8:
                raise StopIteration()
            for ch in range(2):
                (wv,) = stream(w_out[l], 8, [(512 * ch, 512)], key=("out", l, ch))
                for j in range(4):
                    ot = 4 * ch + j
                    pp = ps()
                    for kt in range(8):
                        S.mm(pp[:, 0:N], lhs(wv, kt, j), MIXv[:, kt, :], start=(kt == 0), stop=(kt == 7))
                    gate_resid(pp, l, 2, ot)
            layernorm(l, "ln1_g", "ln1_b")
            if STOP == 9:
                raise StopIteration()
            modulate(Uv, Xv, l, 4, 3)
            for ch in range(11):
                w1, w2 = stream(w_up[l], 8, [(256 * ch, 256), (DFF + 256 * ch, 256)], key=("up", l, ch))
                for j in range(2):
                    jt = 2 * ch + j
                    p1 = ps(); p2 = ps()
                    for kt in range(8):
                        S.mm(p1[:, 0:N], lhs(w1, kt, j), Uv[:, kt, :], start=(kt == 0), stop=(kt == 7))
                    for kt in range(8):
                        S.mm(p2[:, 0:N], lhs(w2, kt, j), Uv[:, kt, :], start=(kt == 0), stop=(kt == 7))
                    S.act(T[1], p1[:, 0:N], AF.Silu)
                    S.tt(HFv[:, jt, :], T[1], p2[:, 0:N], ALU.mult)
            for ot in range(8):
                ((v, _o, _n),) = stream(w_dn[l], 22, [(128 * ot, 128)], key=("dn", l, ot))
                pp = ps()
                for jt in range(22):
                    S.mm(pp[:, 0:N], v[:, jt, :], HFv[:, jt, :], start=(jt == 0), stop=(jt == 21))
                gate_resid(pp, l, 5, ot)
            layernorm(l, "ln2_g", "ln2_b")
        for g in range(N // 128 if not sample else 1):
            rows = 64 if sample else 128
            st = STG[g % 2]
            for half in range(2):
                pp = ps()
                for j in range(4):
                    k = 4 * half + j
                    S.tr(pp[0:rows, 128 * j:128 * j + 128], Xv[:, k, 128 * g:128 * g + rows], ident)
                S.copy(st[0:rows, 512 * half:512 * half + 512], pp[0:rows, :], eng=("scalar" if half else "vector"))
            fin.append(odma(ydst[128 * g:128 * g + rows, :], st[0:rows, :]))

    try:
        for bi in BLOCKS:
            blockrun(bi)
    except StopIteration:
        pass
    S.final_wait("sync", fin)
    S.emit()
    es.close()
    return nc


BLOCKS = [0, 1, 2, 3, 4]
STOP = 0
SKIP_ADA = False
SKIP_S5PRE = False
_cache = {}


def kernel(**inp):
    f32 = lambda a: np.ascontiguousarray(np.asarray(a, dtype=np.float32))
    if "nc" not in _cache:
        nc = bass.Bass("TRN2", target_bir_lowering=False)
        build(nc)
        _cache["nc"] = nc
    nc = _cache["nc"]
    shared = {}
    for nm in ("w_ada", "b_ada", "w_in", "b_in", "w_conv", "b_conv", "w_lru_gates", "b_lru_gates", "lru_lambda", "w_lru_out",
               "s5_a_re", "s5_a_im", "s5_log_dt", "s5_b_re", "s5_b_im", "s5_c_re", "s5_c_im", "s5_d", "w_s5_glu", "w_out",
               "ln1_g", "ln1_b", "w_ffn_up", "w_ffn_down", "ln2_g", "ln2_b"):
        shared[nm] = f32(inp[nm])
    shared["ident"] = np.eye(128, dtype=np.float32)
    shared["mask8"] = np.kron(np.eye(8, dtype=np.float32), np.ones((16, 1), np.float32))
    xp = f32(inp["x_prompt"]); xs = f32(inp["x_sample"]); cp = f32(inp["c_prompt"]); cs = f32(inp["c_sample"])
    sh = f32(inp["state_lru_h"]); scv = f32(inp["state_lru_conv"]); sre = f32(inp["state_s5_re"]); sim = f32(inp["state_s5_im"])
    in_maps = []
    for c in range(NCORES):
        m = dict(shared)
        sl = slice(16 * c, 16 * c + 16)
        m["xp"] = xp[c]
        m["xs"] = np.ascontiguousarray(xs[sl].reshape(64, DM))
        m["call"] = np.ascontiguousarray(np.concatenate([cp[c:c + 1], cs[sl]], axis=0))
        m["st_h"] = np.ascontiguousarray(sh[:, sl]); m["st_cv"] = np.ascontiguousarray(scv[:, sl])
        m["st_re"] = np.ascontiguousarray(sre[:, sl]); m["st_im"] = np.ascontiguousarray(sim[:, sl])
        in_maps.append(m)
    res = run_bass_kernel_spmd(nc, in_maps, core_ids=list(range(NCORES)))
    R = res.results
    cat = lambda nm, ax: np.concatenate([np.asarray(R[c][nm], dtype=np.float32) for c in range(NCORES)], axis=ax)
    stack = lambda nm, ax: np.stack([np.asarray(R[c][nm], dtype=np.float32) for c in range(NCORES)], axis=ax)
    y_p = stack("yp", 0)
    y_s = cat("ys", 0).reshape(128, 4, DM)
    return (y_p, y_s, stack("o_hp", 1), cat("o_hs", 1), stack("o_cp", 1), cat("o_cs", 1),
            stack("o_rp", 1), cat("o_rs", 1), stack("o_ip", 1), cat("o_is", 1))
```
